# Optimizing a Trainium2 kernel written in Bass

```python
import math
import jax, jax.numpy as jnp
from jax import lax
import numpy as np

D_MODEL = 1024
BATCH = 4
SEQ = 8192
DEPTH = 1
DEC_BATCH = 4
DEC_SEQ = 4096
PAST_LEN = 128

GRID_W = 64
HEAD_DIM = 64
N_HEADS_A = 8
N_KV_A = 2
N_HEADS_B = 8
N_KV_B = 2
Q_BLOCK = 128
WINDOW = 128
N_BUCKETS = 32
MAX_DISTANCE = 128
ROPE_THETA = 10000.0
AXIS_DIM = HEAD_DIM // 2
D_FF = 2816
EPS = 1e-6
NEG_INF = -1e30
SPLIT_SIZES = (N_HEADS_A * HEAD_DIM, N_KV_A * HEAD_DIM, N_KV_A * HEAD_DIM,
               N_HEADS_B * HEAD_DIM, N_KV_B * HEAD_DIM, N_KV_B * HEAD_DIM,
               D_MODEL, D_MODEL)
IN_COLS = sum(SPLIT_SIZES)

kernel_name = "hybrid_axial_window_encoder"


def rmsnorm(x, g):
    xf = x.astype(jnp.float32)
    y = xf * lax.rsqrt(jnp.mean(xf * xf, axis=-1, keepdims=True) + EPS) * g.astype(jnp.float32)
    return y.astype(x.dtype)


def swiglu_ffn(x, w_in, w_out):
    a, b = jnp.split(x @ w_in, 2, axis=-1)
    return (jax.nn.silu(a) * b) @ w_out


def t5_bucket(rel):
    nb = N_BUCKETS // 2
    max_exact = nb // 2
    ret = jnp.where(rel > 0, nb, 0)
    n = jnp.abs(rel)
    large = max_exact + (jnp.log(jnp.maximum(n, 1).astype(jnp.float32) / max_exact)
                         / math.log(MAX_DISTANCE / max_exact) * (nb - max_exact)).astype(jnp.int32)
    large = jnp.minimum(large, nb - 1)
    return ret + jnp.where(n < max_exact, n, large)


def axial_rope(x):
    B, T, H, _ = x.shape
    rows = T // GRID_W
    grid_r, grid_c = jnp.meshgrid(jnp.arange(rows, dtype=jnp.float32),
                                  jnp.arange(GRID_W, dtype=jnp.float32), indexing="ij")
    pos = jnp.stack([grid_r.reshape(-1), grid_c.reshape(-1)], axis=-1)
    inv = ROPE_THETA ** (-jnp.arange(0, AXIS_DIM, 2, dtype=jnp.float32) / AXIS_DIM)
    ang = pos[:, :, None] * inv
    cos = jnp.cos(ang)[None, :, None]
    sin = jnp.sin(ang)[None, :, None]
    xf = x.astype(jnp.float32).reshape(B, T, H, 2, 2, AXIS_DIM // 2)
    x1 = xf[..., 0, :]
    x2 = xf[..., 1, :]
    out = jnp.stack([x1 * cos - x2 * sin, x2 * cos + x1 * sin], axis=-2)
    return out.reshape(x.shape).astype(x.dtype)


def dense_attention_blocks(q, k, v):
    B, T, KV, G, D = q.shape
    nb = T // Q_BLOCK
    scale = 1.0 / math.sqrt(D)
    kf = k.astype(jnp.float32)
    vf = v.astype(jnp.float32)
    qb = q.reshape(B, nb, Q_BLOCK, KV, G, D).transpose(1, 0, 2, 3, 4, 5)

    def one_block(qblk):
        s = jnp.einsum("bqkgd,bskd->bkgqs", qblk.astype(jnp.float32), kf) * scale
        p = jax.nn.softmax(s, axis=-1)
        return jnp.einsum("bkgqs,bskd->bqkgd", p, vf)

    o = lax.map(one_block, qb)
    return o.transpose(1, 0, 2, 3, 4, 5).reshape(B, T, KV * G * D).astype(q.dtype)


def window_sink_attention(q, k, v, rel_bias, sink):
    B, T, KV, G, D = q.shape
    nb = T // Q_BLOCK
    scale = 1.0 / math.sqrt(D)
    pad = ((0, 0), (Q_BLOCK, Q_BLOCK), (0, 0), (0, 0))
    kp = jnp.pad(k.astype(jnp.float32), pad).reshape(B, nb + 2, Q_BLOCK, KV, D)
    vp = jnp.pad(v.astype(jnp.float32), pad).reshape(B, nb + 2, Q_BLOCK, KV, D)
    k_band = jnp.concatenate([kp[:, :-2], kp[:, 1:-1], kp[:, 2:]], axis=2)
    v_band = jnp.concatenate([vp[:, :-2], vp[:, 1:-1], vp[:, 2:]], axis=2)
    qb = q.astype(jnp.float32).reshape(B, nb, Q_BLOCK, KV, G, D)
    rel = jnp.arange(3 * Q_BLOCK)[None, :] - Q_BLOCK - jnp.arange(Q_BLOCK)[:, None]
    bias = rel_bias.astype(jnp.float32)[t5_bucket(rel)]
    bias = bias.transpose(2, 0, 1).reshape(KV, G, Q_BLOCK, 3 * Q_BLOCK)
    key_pos = jnp.arange(nb)[:, None] * Q_BLOCK - Q_BLOCK + jnp.arange(3 * Q_BLOCK)[None, :]
    in_range = (key_pos >= 0) & (key_pos < T)
    mask = (jnp.abs(rel) <= WINDOW)[None] & in_range[:, None, :]
    s = jnp.einsum("bnqkgd,bnskd->bnkgqs", qb, k_band) * scale + bias[None, None]
    s = jnp.where(mask[None, :, None, None], s, NEG_INF)
    sink_col = jnp.broadcast_to(sink.astype(jnp.float32).reshape(1, 1, KV, G, 1, 1),
                                s.shape[:-1] + (1,))
    p = jax.nn.softmax(jnp.concatenate([s, sink_col], axis=-1), axis=-1)[..., :-1]
    o = jnp.einsum("bnkgqs,bnskd->bnqkgd", p, v_band)
    return o.reshape(B, T, KV * G * D).astype(q.dtype)


def token_mixing(h, w_in, q_norm_a, k_norm_a, sink_b, w_branch_a, w_branch_b, w_out, rel_bias):
    B, T, _ = h.shape
    proj = h @ w_in
    offsets = list(np.cumsum(SPLIT_SIZES)[:-1])
    qa, ka, va, qb, kb, vb, ga, gb = jnp.split(proj, offsets, axis=-1)
    qa = axial_rope(rmsnorm(qa.reshape(B, T, N_HEADS_A, HEAD_DIM), q_norm_a))
    ka = axial_rope(rmsnorm(ka.reshape(B, T, N_KV_A, HEAD_DIM), k_norm_a))
    qa = qa.reshape(B, T, N_KV_A, N_HEADS_A // N_KV_A, HEAD_DIM)
    va = va.reshape(B, T, N_KV_A, HEAD_DIM)
    ya = dense_attention_blocks(qa, ka, va) @ w_branch_a
    qb = qb.reshape(B, T, N_KV_B, N_HEADS_B // N_KV_B, HEAD_DIM)
    kb = kb.reshape(B, T, N_KV_B, HEAD_DIM)
    vb = vb.reshape(B, T, N_KV_B, HEAD_DIM)
    yb = window_sink_attention(qb, kb, vb, rel_bias, sink_b) @ w_branch_b
    merged = jax.nn.sigmoid(ga) * ya + jax.nn.sigmoid(gb) * yb
    return merged @ w_out


def trunk(x, norm_ffn1, w_ffn1_in, w_ffn1_out, norm_mix, w_in, q_norm_a, k_norm_a, sink_b,
          w_branch_a, w_branch_b, w_out, norm_ffn2, w_ffn2_in, w_ffn2_out, rel_bias, norm_final):
    for l in range(DEPTH):
        x = x + 0.5 * swiglu_ffn(rmsnorm(x, norm_ffn1[l]), w_ffn1_in[l], w_ffn1_out[l])
        x = x + token_mixing(rmsnorm(x, norm_mix[l]), w_in[l], q_norm_a[l], k_norm_a[l], sink_b[l],
                             w_branch_a[l], w_branch_b[l], w_out[l], rel_bias)
        x = x + 0.5 * swiglu_ffn(rmsnorm(x, norm_ffn2[l]), w_ffn2_in[l], w_ffn2_out[l])
    return rmsnorm(x, norm_final)


def setup_inputs(seed: int = 0) -> dict:
    key = jax.random.key(seed)
    ks = jax.random.split(key, 20)
    f32 = jnp.float32

    def nrm(k, shape, scale):
        return jax.random.normal(k, shape, f32) * scale

    def gain(k, shape):
        return 1.0 + 0.05 * jax.random.normal(k, shape, f32)

    wa = N_HEADS_A * HEAD_DIM
    wb = N_HEADS_B * HEAD_DIM
    return {
        "x_prompt": jax.random.normal(ks[0], (BATCH, SEQ, D_MODEL), f32),
        "x_sample": jax.random.normal(ks[1], (DEC_BATCH, DEC_SEQ, D_MODEL), f32),
        "norm_ffn1": gain(ks[2], (DEPTH, D_MODEL)),
        "w_ffn1_in": nrm(ks[3], (DEPTH, D_MODEL, 2 * D_FF), D_MODEL ** -0.5),
        "w_ffn1_out": nrm(ks[4], (DEPTH, D_FF, D_MODEL), D_FF ** -0.5),
        "norm_mix": gain(ks[5], (DEPTH, D_MODEL)),
        "w_in": nrm(ks[6], (DEPTH, D_MODEL, IN_COLS), D_MODEL ** -0.5),
        "q_norm_a": gain(ks[7], (DEPTH, HEAD_DIM)),
        "k_norm_a": gain(ks[8], (DEPTH, HEAD_DIM)),
        "sink_b": nrm(ks[9], (DEPTH, N_HEADS_B), 0.5),
        "w_branch_a": nrm(ks[10], (DEPTH, wa, D_MODEL), wa ** -0.5),
        "w_branch_b": nrm(ks[11], (DEPTH, wb, D_MODEL), wb ** -0.5),
        "w_out": nrm(ks[12], (DEPTH, D_MODEL, D_MODEL), D_MODEL ** -0.5),
        "norm_ffn2": gain(ks[13], (DEPTH, D_MODEL)),
        "w_ffn2_in": nrm(ks[14], (DEPTH, D_MODEL, 2 * D_FF), D_MODEL ** -0.5),
        "w_ffn2_out": nrm(ks[15], (DEPTH, D_FF, D_MODEL), D_FF ** -0.5),
        "rel_bias": nrm(ks[16], (N_BUCKETS, N_HEADS_B), 0.1),
        "norm_final": gain(ks[17], (D_MODEL,)),
    }


def reference(x_prompt, x_sample, norm_ffn1, w_ffn1_in, w_ffn1_out, norm_mix, w_in, q_norm_a, k_norm_a,
              sink_b, w_branch_a, w_branch_b, w_out, norm_ffn2, w_ffn2_in, w_ffn2_out, rel_bias, norm_final):
    y_prompt = trunk(x_prompt, norm_ffn1, w_ffn1_in, w_ffn1_out, norm_mix, w_in, q_norm_a, k_norm_a, sink_b,
                     w_branch_a, w_branch_b, w_out, norm_ffn2, w_ffn2_in, w_ffn2_out, rel_bias, norm_final)
    y_sample = trunk(x_sample, norm_ffn1, w_ffn1_in, w_ffn1_out, norm_mix, w_in, q_norm_a, k_norm_a, sink_b,
                     w_branch_a, w_branch_b, w_out, norm_ffn2, w_ffn2_in, w_ffn2_out, rel_bias, norm_final)
    return (y_prompt, y_sample)
```

```python
import math
import os
import numpy as np
import ml_dtypes
import concourse.bass as bass
import concourse.mybir as mybir
from concourse.bass_utils import run_bass_kernel_spmd

F32 = mybir.dt.float32
BF16 = mybir.dt.bfloat16
AF = mybir.ActivationFunctionType
ALU = mybir.AluOpType
AX = mybir.AxisListType

D = 1024
DFF = 2816
NJ = DFF // 128
SEQ_P = 8192
SEQ_S = 4096
HP = SEQ_P // 2
HS = SEQ_S // 2
NM = HP + HS
NTOT = 2 * NM
TA = 384
EPS = 1e-6
QPERM = [0, 4, 1, 5, 2, 6, 3, 7]
NEG = -30000.0

DEBUG = os.environ.get("MK_DEBUG", "") != ""


class Op:
    __slots__ = ("stream", "fn", "is_dma", "key", "deps", "signal", "val", "sem", "idx")


class Prog:
    COMPUTE = ("pe", "act", "dve", "pool")

    def __init__(self, nc):
        self.nc = nc
        self.ops = []
        self.lastw = {}
        self.readers = {}
        self.dma_cnt = {}
        self.bar_start = 0
        self.bar_deps = []
        self.bar_pending = set()

    def barrier(self):
        deps = {}
        for op in self.ops[self.bar_start:]:
            if op.is_dma:
                deps[("d", op.key)] = op
            else:
                deps[op.stream] = op
        for o in self.bar_deps:
            k = ("d", o.key) if o.is_dma else o.stream
            deps.setdefault(k, o)
        self.bar_deps = list(deps.values())
        for o in self.bar_deps:
            o.signal = True
        self.bar_start = len(self.ops)
        self.bar_pending = {"pe", "act", "dve", "pool", "sp"}

    def add(self, stream, fn, reads=(), writes=(), dma_key=None):
        op = Op()
        op.stream = stream
        op.fn = fn
        op.is_dma = dma_key is not None
        op.key = dma_key
        op.signal = op.is_dma
        op.val = 0
        op.sem = None
        op.idx = len(self.ops)
        deps = []
        for r in reads:
            w = self.lastw.get(r)
            if w is not None:
                deps.append((w, "raw"))
            if isinstance(r, str) and r[0] == "P" and r[1:].isdigit():
                rd = self.readers.get(r)
                if rd:
                    for k, o in rd.items():
                        if k != "dma" and k != stream:
                            deps.append((o, "war"))
        for r in writes:
            w = self.lastw.get(r)
            if w is not None:
                deps.append((w, "waw"))
            rd = self.readers.get(r)
            if rd:
                for k, o in rd.items():
                    if k == "dma":
                        for oo in o:
                            deps.append((oo, "war"))
                    else:
                        deps.append((o, "war"))
        keep = []
        seen = set()
        for (o, kind) in deps:
            if o is op or id(o) in seen:
                continue
            if (not o.is_dma) and (not op.is_dma) and o.stream == op.stream:
                if kind != "raw" or op.stream == "pe":
                    continue
            seen.add(id(o))
            o.signal = True
            keep.append(o)
        if stream in self.bar_pending:
            self.bar_pending.discard(stream)
            for o in self.bar_deps:
                if id(o) in seen:
                    continue
                if (not o.is_dma) and (not op.is_dma) and o.stream == op.stream == "pe":
                    continue
                seen.add(id(o))
                keep.append(o)
        op.deps = keep
        for r in reads:
            rd = self.readers.setdefault(r, {})
            if op.is_dma:
                rd.setdefault("dma", []).append(op)
            else:
                rd[op.stream] = op
        for r in writes:
            self.lastw[r] = op
            self.readers[r] = {}
        if op.is_dma:
            c = self.dma_cnt.get(dma_key, 0) + 1
            self.dma_cnt[dma_key] = c
            op.val = 16 * c
        self.ops.append(op)
        return op

    def emit(self, final_waits):
        nc = self.nc
        sems = {}

        def get_sem(name):
            if name not in sems:
                sems[name] = nc.alloc_semaphore("s_" + name.replace("/", "_"))
            return sems[name]

        LIM = 30000
        cnt = {s: 0 for s in self.COMPUTE}
        for op in self.ops:
            if op.is_dma:
                op.sem = get_sem("d_" + str(op.key))
            elif op.signal:
                c = cnt[op.stream]
                op.sem = get_sem("%s%d" % (op.stream, c // LIM))
                op.val = c % LIM + 1
                cnt[op.stream] = c + 1
        self.nsems = len(sems)
        streams = {}
        for op in self.ops:
            streams.setdefault(op.stream, []).append(op)
        fin = list(final_waits)

        def run_stream(e, ops, is_last_stream):
            waited = {}
            for op in ops:
                for d in op.deps:
                    k = id(d.sem)
                    if waited.get(k, 0) >= d.val:
                        continue
                    e.wait_ge(d.sem, d.val)
                    waited[k] = d.val
                ins = op.fn(e)
                if op.signal:
                    ins.then_inc(op.sem, 16 if op.is_dma else 1)
            if is_last_stream:
                for o in fin:
                    e.wait_ge(o.sem, o.val)

        with nc.Block() as block:
            if "sp" in streams:
                @block.sync
                def _(e):
                    run_stream(e, streams["sp"], True)
            if "pool" in streams:
                @block.gpsimd
                def _(e):
                    run_stream(e, streams["pool"], False)
            if "act" in streams:
                @block.scalar
                def _(e):
                    run_stream(e, streams["act"], False)
            if "dve" in streams:
                @block.vector
                def _(e):
                    run_stream(e, streams["dve"], False)
            if "pe" in streams:
                @block.tensor
                def _(e):
                    run_stream(e, streams["pe"], False)


def _rw(args):
    return [a[1] for a in args if a is not None and isinstance(a, tuple)]


def _ap(a):
    return a[0] if isinstance(a, tuple) else a


def _t5_bucket_np(rel):
    try:
        import jax
        import jax.numpy as jnp
        with jax.default_device(jax.devices("cpu")[0]):
            r = jnp.asarray(rel, dtype=jnp.int32)
            nb = 16
            max_exact = 8
            ret = jnp.where(r > 0, nb, 0)
            n = jnp.abs(r)
            large = max_exact + (jnp.log(jnp.maximum(n, 1).astype(jnp.float32) / max_exact)
                                 / math.log(128 / max_exact) * (nb - max_exact)).astype(jnp.int32)
            large = jnp.minimum(large, nb - 1)
            out = ret + jnp.where(n < max_exact, n, large)
            return np.asarray(out)
    except Exception:
        rel = np.asarray(rel, dtype=np.int64)
        nb, max_exact = 16, 8
        ret = np.where(rel > 0, nb, 0)
        n = np.abs(rel)
        large = max_exact + (np.log(np.maximum(n, 1).astype(np.float32) / np.float32(max_exact))
                             / np.float32(math.log(128 / max_exact)) * np.float32(nb - max_exact)).astype(np.int32)
        large = np.minimum(large, nb - 1)
        return ret + np.where(n < max_exact, n, large)


def _bias_structure():
    kk = np.arange(128)[:, None]
    qq = np.arange(128)[None, :]
    lst = []
    tiles = []
    masks = []
    for o in range(3):
        rel = (o - 1) * 128 + kk - qq
        bk = _t5_bucket_np(rel)
        valid = np.abs(rel) <= 128
        for b in range(32):
            m = (bk == b) & valid
            if m.any():
                lst.append((o, b))
                tiles.append(m.astype(np.float32) * 8.0)
        if o != 1:
            masks.append(np.where(valid, 0.0, NEG).astype(np.float32))
    return lst, np.stack(tiles), np.stack(masks)


_OH_LIST, _OH_TILES, _MASK_TILES = None, None, None


def _get_bias_structure():
    global _OH_LIST, _OH_TILES, _MASK_TILES
    if _OH_LIST is None:
        _OH_LIST, _OH_TILES, _MASK_TILES = _bias_structure()
    return _OH_LIST, _OH_TILES, _MASK_TILES


def _rope_tables(pos):
    pos = np.asarray(pos, dtype=np.int64)
    row = (pos // 64).astype(np.float32)
    col = (pos % 64).astype(np.float32)
    inv = (np.float32(10000.0) ** (-np.arange(0, 32, 2, dtype=np.float32) / np.float32(32))).astype(np.float32)
    CC = np.zeros((len(pos), 64), np.float32)
    SS = np.zeros((len(pos), 64), np.float32)
    for a, p in enumerate((row, col)):
        ang = (p[:, None] * inv[None, :]).astype(np.float32)
        c = np.cos(ang).astype(np.float32)
        s = np.sin(ang).astype(np.float32)
        CC[:, a * 32:a * 32 + 16] = c
        CC[:, a * 32 + 16:a * 32 + 32] = c
        SS[:, a * 32:a * 32 + 16] = -s
        SS[:, a * 32 + 16:a * 32 + 32] = s
    return CC, SS


SB_BASE = 16512 + 64
SB_LIMIT = 229376


def build_program(phases="ABCD"):
    oh_list, _, _ = _get_bias_structure()
    NOH = len(oh_list)
    nc = bass.Bass("TRN2", target_bir_lowering=False)
    P = Prog(nc)

    def din(name, shape, dt=F32):
        return nc.dram_tensor(name, list(shape), dt, kind="ExternalInput").ap()

    def dscr(name, shape, dt):
        kind = "ExternalOutput" if DEBUG else "Internal"
        return nc.dram_tensor(name, list(shape), dt, kind=kind).ap()

    xm = din("xm", [NM, D])
    xp = din("xp", [NM, D])
    w1 = din("w1", [D, 2 * DFF])
    w2 = din("w2", [DFF, D])
    w3 = din("w3", [D, 2 * DFF])
    w4 = din("w4", [DFF, D])
    win = din("win", [D, 3584])
    wba = din("wba", [512, D])
    wbb = din("wbb", [512, D])
    wout = din("wout", [D, D])
    gcols = din("gcols", [128, 3, 8])
    gfin = din("gfin", [D])
    qg = din("qg", [4, 64])
    sink = din("sink", [8])
    relb = din("relb", [256])
    ropeC = din("ropeC", [NTOT, 64])
    ropeS = din("ropeS", [NTOT, 64])
    identd = din("identd", [128, 128])
    ohd = din("ohd", [NOH, 128, 128])
    maskd = din("maskd", [2, 128, 128])
    validd = din("validd", [128, 2])
    yout = nc.dram_tensor("y", [NM, D], F32, kind="ExternalOutput").ap()

    x1s = dscr("x1s", [NM, D], F32)
    h2s = dscr("h2s", [8, 128, NTOT], BF16)
    kaTs = dscr("kaTs", [128, NTOT], BF16)
    kbTs = dscr("kbTs", [128, NTOT], BF16)
    vaAs = dscr("vaAs", [NTOT, 2, 128], BF16)
    vaBs = dscr("vaBs", [NTOT, 2, 128], BF16)
    qaTs = dscr("qaTs", [4, 128, NM], BF16)
    qbTs = dscr("qbTs", [4, 128, NM], BF16)
    sgs = dscr("sgs", [16, 128, NM], BF16)
    x2s = dscr("x2s", [NM, D], F32)

    PP = [nc.alloc_psum_tensor("pp%d" % i, [128, 1024], F32) for i in range(4)]

    def pbank(i):
        return PP[i // 2][:, (i % 2) * 512:(i % 2 + 1) * 512]

    def pbank_bf(i):
        return PP[i // 2][:].bitcast(BF16)[:, (i % 2) * 1024:(i % 2 + 1) * 1024]

    arena = {"pers": SB_BASE, "cur": SB_BASE, "n": 0}

    def sb(name, shape, dt, persistent=False):
        nbytes = int(np.prod(shape[1:])) * (4 if dt == F32 else 2)
        nbytes = (nbytes + 63) // 64 * 64
        off = arena["cur"]
        assert off + nbytes <= SB_LIMIT, ("SBUF overflow", name, off, nbytes)
        arena["cur"] = off + nbytes
        arena["n"] += 1
        h = nc.alloc_sbuf_tensor_at("%s_%d" % (name, arena["n"]), list(shape), dt, offset=off)
        if persistent:
            arena["pers"] = arena["cur"]
        return h

    def phase_reset():
        arena["cur"] = arena["pers"]

    ident = sb("ident", [128, 128], BF16, True)
    gcol_sb = sb("gcol_sb", [128, 3, 8], F32, True)
    eps_sb = sb("eps_sb", [128, 1], F32, True)
    junk = sb("junk", [128, 1024], BF16, True)
    stat = sb("stat", [128, 256], F32, True)

    def dma(stream, out, in_, key):
        o, i = _ap(out), _ap(in_)
        return P.add(stream, lambda e: e.dma_start(out=o, in_=i), reads=_rw([in_]), writes=_rw([out]),
                     dma_key=key)

    def mm(out, lhsT, rhs, start, stop):
        o, l, r = _ap(out), _ap(lhsT), _ap(rhs)
        return P.add("pe", lambda e: e.matmul(o, lhsT=l, rhs=r, start=start, stop=stop),
                     reads=_rw([lhsT, rhs]), writes=_rw([out]))

    def tr(out, in_):
        o, i = _ap(out), _ap(in_)
        idn = ident[:]
        return P.add("pe", lambda e: e.transpose(o, i, idn), reads=_rw([in_]) + ["ident"], writes=_rw([out]))

    def act(out, in_, func, scale=1.0, bias=None, accum=None):
        o, i = _ap(out), _ap(in_)
        sc = _ap(scale) if isinstance(scale, tuple) else scale
        kw = {}
        if bias is not None:
            kw["bias"] = _ap(bias)
        if accum is not None:
            kw["accum_out"] = _ap(accum)
        rd = _rw([in_]) + (_rw([scale]) if isinstance(scale, tuple) else []) + (_rw([bias]) if bias is not None else [])
        wr = _rw([out]) + (_rw([accum]) if accum is not None else [])
        return P.add("act", lambda e: e.activation(out=o, in_=i, func=func, scale=sc, **kw), reads=rd, writes=wr)

    def tt(eng, out, in0, in1, op):
        o, a, b = _ap(out), _ap(in0), _ap(in1)
        return P.add(eng, lambda e: e.tensor_tensor(out=o, in0=a, in1=b, op=op), reads=_rw([in0, in1]),
                     writes=_rw([out]))

    def ts(eng, out, in0, s1, s2, op0, op1=None):
        o, a = _ap(out), _ap(in0)
        s1a = _ap(s1) if isinstance(s1, tuple) else s1
        s2a = _ap(s2) if isinstance(s2, tuple) else s2
        rd = _rw([in0]) + (_rw([s1]) if isinstance(s1, tuple) else []) + (_rw([s2]) if isinstance(s2, tuple) else [])
        if op1 is None:
            return P.add(eng, lambda e: e.tensor_scalar(out=o, in0=a, scalar1=s1a, scalar2=None, op0=op0),
                         reads=rd, writes=_rw([out]))
        return P.add(eng, lambda e: e.tensor_scalar(out=o, in0=a, scalar1=s1a, scalar2=s2a, op0=op0, op1=op1),
                     reads=rd, writes=_rw([out]))

    def stt(eng, out, in0, scalar, in1, op0, op1):
        o, a, b = _ap(out), _ap(in0), _ap(in1)
        s = _ap(scalar) if isinstance(scalar, tuple) else scalar
        rd = _rw([in0, in1]) + (_rw([scalar]) if isinstance(scalar, tuple) else [])
        return P.add(eng, lambda e: e.scalar_tensor_tensor(out=o, in0=a, scalar=s, in1=b, op0=op0, op1=op1),
                     reads=rd, writes=_rw([out]))

    def cp(eng, out, in_):
        o, i = _ap(out), _ap(in_)
        if eng == "act":
            return P.add("act", lambda e: e.copy(out=o, in_=i), reads=_rw([in_]), writes=_rw([out]))
        return P.add(eng, lambda e: e.tensor_copy(out=o, in_=i), reads=_rw([in_]), writes=_rw([out]))

    def recip(out, in_):
        o, i = _ap(out), _ap(in_)
        return P.add("dve", lambda e: e.reciprocal(out=o, in_=i), reads=_rw([in_]), writes=_rw([out]))

    def red(out, in_):
        o, i = _ap(out), _ap(in_)
        return P.add("dve", lambda e: e.tensor_reduce(out=o, in_=i, axis=AX.X, op=ALU.add), reads=_rw([in_]),
                     writes=_rw([out]))

    def mset(eng, out, val):
        o = _ap(out)
        return P.add(eng, lambda e: e.memset(o, val), reads=[], writes=_rw([out]))

    final_ops = []

    dma("pool", (ident[:], "ident"), identd, "c0")
    dma("sp", (gcol_sb[:], "gcol"), gcols, "c1")
    mset("dve", (eps_sb[:], "eps"), EPS)

    Bt = sb("Bt", [128, 3, 8, 128], BF16, True)
    Bacc = sb("Bacc", [128, 8, 128], F32, True)
    rb_bc = sb("rb_bc", [128, 256], F32, True)
    ohb = [sb("ohb%d" % i, [128, 128], BF16, True) for i in range(2)]
    mskb = sb("mskb", [128, 2, 128], F32, True)
    dma("sp", (rb_bc[:], "rb_bc"), relb.partition_broadcast(128), "c6")
    dma("sp", (mskb[:], "mskb"), maskd.rearrange("m p q -> p m q"), "c7")
    n_ = 0
    for o in range(3):
        mset("dve", (Bacc[:], "Bacc"), 0.0)
        for idx_, (oo, bk) in enumerate(oh_list):
            if oo != o:
                continue
            ob = n_ % 2
            n_ += 1
            dma("pool", (ohb[ob][:], "ohb%d" % ob), ohd[idx_], "coh%d" % ob)
            for h in range(8):
                stt("dve", (Bacc[:, h, :], "Bacc"), (ohb[ob][:], "ohb%d" % ob),
                    (rb_bc[:, bk * 8 + h:bk * 8 + h + 1], "rb_bc"), (Bacc[:, h, :], "Bacc"), ALU.mult, ALU.add)
        if o != 1:
            mi = 0 if o == 0 else 1
            tt("dve", (Bacc[:], "Bacc"), (Bacc[:], "Bacc"),
               (mskb[:, mi, :].unsqueeze(1).broadcast_to([128, 8, 128]), "mskb"), ALU.add)
        cp("dve", (Bt[:, o, :, :], "Bt"), (Bacc[:], "Bacc"))

    def rstd_rows(xin, xkey, s, n):
        act((junk[:, 0:n], "junk"), (xin, xkey), AF.Square, accum=(stat[:, s:s + 1], "st_ss%d" % s))
        act((stat[:, 8 + s:9 + s], "st_sq%d" % s), (stat[:, s:s + 1], "st_ss%d" % s), AF.Sqrt,
            scale=1.0 / n, bias=(eps_sb[:], "eps"))
        recip((stat[:, 16 + s:17 + s], "st_r%d" % s), (stat[:, 8 + s:9 + s], "st_sq%d" % s))
        return (stat[:, 16 + s:17 + s], "st_r%d" % s)

    def ffn_phase(tag, srcs, wA, wB, gidx, post):
        phase_reset()
        W1b = sb(tag + "W1b", [128, 8, 2 * DFF], BF16)
        W2b = sb(tag + "W2b", [128, NJ, D], BF16)
        nsub = TA // 128
        xt = [sb(tag + "xt%d" % i, [128, nsub, D], F32) for i in range(2)]
        hrow = [sb(tag + "hrow%d" % i, [128, D], BF16) for i in range(2)]
        hT = sb(tag + "hT", [128, 8, TA], BF16)
        h2T = sb(tag + "h2T", [128, 8, TA], BF16) if post == "A" else None
        gT = sb(tag + "gT", [128, NJ, TA], BF16)
        sa = [sb(tag + "sa%d" % i, [128, TA], F32) for i in range(2)]
        if post == "D":
            gfin_sb = sb(tag + "gfin", [128, D], F32)
            dma("sp", (gfin_sb[:], tag + "gfin"), gfin.partition_broadcast(128), "c2")
        w1v = wA.rearrange("(k p) f -> p k f", p=128)
        w2v = wB.rearrange("(j p) d -> p j d", p=128)
        for k in range(8):
            for hf in range(2):
                dma("pool", (W1b[:, k, hf * DFF:(hf + 1) * DFF], tag + "W1b"), w1v[:, k, hf * DFF:(hf + 1) * DFF],
                    tag + "w1")
        for j0 in range(0, NJ, 6):
            j1 = min(NJ, j0 + 6)
            dma("pool", (W2b[:, j0:j1, :], tag + "W2b"), w2v[:, j0:j1, :], tag + "w2")

        tiles = []
        for (src, ntok, mine, soff) in srcs:
            for t0 in range(0, ntok, TA):
                tiles.append((src, t0, mine, soff + t0))
        nt = len(tiles)

        def load_x(ti):
            src, t0, mine, so = tiles[ti]
            b = ti % 2
            dma("sp", (xt[b][:], tag + "xt%d" % b), src[t0:t0 + TA, :].rearrange("(s p) d -> p s d", p=128),
                tag + "x%d" % b)

        def norm_to_T(xbuf, xkey, g_i, dstT, dkey):
            for s in range(nsub):
                hb = s % 2
                r = rstd_rows(xbuf[:, s, :], xkey, s, D)
                ts("dve", (hrow[hb][:], tag + "hrow%d" % hb), (xbuf[:, s, :], xkey), r, None, ALU.mult)
                tb = 6 + (s % 2)
                pv = pbank_bf(tb).rearrange("p (k t) -> p k t", k=8)
                for k in range(8):
                    tr((pv[:, k, :], "P%d" % tb), (hrow[hb][:, k * 128:(k + 1) * 128], tag + "hrow%d" % hb))
                gb = gcol_sb[:, g_i, :].unsqueeze(2).broadcast_to([128, 8, 128])
                tt("dve", (dstT[:, :, s * 128:(s + 1) * 128], dkey), (pv, "P%d" % tb), (gb, "gcol"), ALU.mult)

        load_x(0)
        if nt > 1:
            load_x(1)
        for ti in range(nt):
            src, t0, mine, so = tiles[ti]
            b = ti % 2
            xkey = tag + "xt%d" % b
            norm_to_T(xt[b], xkey, gidx, hT, tag + "hT")
            for j in range(NJ):
                pa = (2 * j) % 4
                pbk = pa + 1
                for k in range(8):
                    mm((pbank(pa)[:, 0:TA], "P%d" % pa), (W1b[:, k, j * 128:(j + 1) * 128], tag + "W1b"),
                       (hT[:, k, :], tag + "hT"), k == 0, k == 7)
                for k in range(8):
                    mm((pbank(pbk)[:, 0:TA], "P%d" % pbk),
                       (W1b[:, k, DFF + j * 128:DFF + (j + 1) * 128], tag + "W1b"),
                       (hT[:, k, :], tag + "hT"), k == 0, k == 7)
                sb_ = j % 2
                act((sa[sb_][:], tag + "sa%d" % sb_), (pbank(pa)[:, 0:TA], "P%d" % pa), AF.Silu)
                tt("dve", (gT[:, j, :], tag + "gT"), (pbank(pbk)[:, 0:TA], "P%d" % pbk),
                   (sa[sb_][:], tag + "sa%d" % sb_), ALU.mult)
            for s in range(nsub):
                for hf in range(2):
                    po = 4 + ((2 * s + hf) % 2)
                    for j in range(NJ):
                        mm((pbank(po), "P%d" % po), (gT[:, j, s * 128:(s + 1) * 128], tag + "gT"),
                           (W2b[:, j, hf * 512:(hf + 1) * 512], tag + "W2b"), j == 0, j == NJ - 1)
                    stt("dve", (xt[b][:, s, hf * 512:(hf + 1) * 512], xkey), (pbank(po), "P%d" % po), 0.5,
                        (xt[b][:, s, hf * 512:(hf + 1) * 512], xkey), ALU.mult, ALU.add)
            if post == "A":
                if mine:
                    dma("pool", x1s[so:so + TA, :].rearrange("(s p) d -> p s d", p=128), (xt[b][:], xkey),
                        tag + "sx%d" % b)
                norm_to_T(xt[b], xkey, 1, h2T, tag + "h2T")
                dma("pool", h2s[:, :, so:so + TA].rearrange("k p t -> p k t"), (h2T[:], tag + "h2T"), tag + "sh")
            else:
                for s in range(nsub):
                    r = rstd_rows(xt[b][:, s, :], xkey, s, D)
                    stt("dve", (xt[b][:, s, :], xkey), (xt[b][:, s, :], xkey), r,
                        (gfin_sb[:], tag + "gfin"), ALU.mult, ALU.mult)
                o = dma("pool", yout[so:so + TA, :].rearrange("(s p) d -> p s d", p=128), (xt[b][:], xkey),
                        tag + "sy%d" % b)
                final_ops.append(o)
            if ti + 2 < nt:
                load_x(ti + 2)
        P.barrier()

    def phase_b():
        phase_reset()
        TB = 512
        winb = sb("winb", [128, 8, 3584], BF16)
        h2t = [sb("h2t%d" % i, [128, 8, TB], BF16) for i in range(2)]
        qaT_t = [sb("qaT_t%d" % i, [128, 4, TB], BF16) for i in range(2)]
        qbT_t = [sb("qbT_t%d" % i, [128, 4, TB], BF16) for i in range(2)]
        kaT_t = [sb("kaT_t%d" % i, [128, TB], BF16) for i in range(2)]
        kbT_t = [sb("kbT_t%d" % i, [128, TB], BF16) for i in range(2)]
        vA_t = [sb("vA_t%d" % i, [128, 4, 2, 128], BF16) for i in range(2)]
        vB_t = [sb("vB_t%d" % i, [128, 4, 2, 128], BF16) for i in range(2)]
        sg_t = [sb("sg_t%d" % i, [128, 16, TB], BF16) for i in range(2)]
        rC = [sb("rC%d" % i, [128, 4, 64], F32) for i in range(2)]
        rS = [sb("rS%d" % i, [128, 4, 64], F32) for i in range(2)]
        rCq = [sb("rCq%d" % i, [128, 4, 64], F32) for i in range(2)]
        rSq = [sb("rSq%d" % i, [128, 4, 64], F32) for i in range(2)]
        rCk = [sb("rCk%d" % i, [128, 4, 64], F32) for i in range(2)]
        rSk = [sb("rSk%d" % i, [128, 4, 64], F32) for i in range(2)]
        qg_sb = sb("qg_sb", [128, 4, 64], F32)
        xq = [sb("xq%d" % i, [128, 512], F32) for i in range(4)]
        xk = [sb("xk%d" % i, [128, 128], F32) for i in range(4)]
        wsq = sb("wsq", [128, 512], F32)
        wxn = [sb("wxn%d" % i, [128, 512], F32) for i in range(2)]
        wt = sb("wt", [128, 512], F32)
        wu = sb("wu", [128, 512], F32)
        qr = [sb("qr%d" % i, [128, 512], BF16) for i in range(4)]
        ksq = sb("ksq", [128, 128], F32)
        kxn = [sb("kxn%d" % i, [128, 128], F32) for i in range(2)]
        kt_ = sb("kt_", [128, 128], F32)
        ku = sb("ku", [128, 128], F32)
        kr = [sb("kr%d" % i, [128, 128], BF16) for i in range(4)]

        winv = win.rearrange("(k p) f -> p k f", p=128)
        for k in range(8):
            dma("pool", (winb[:, k, :], "winb"), winv[:, k, :], "bw")
        dma("sp", (qg_sb[:], "qg_sb"), bass.AP(qg.tensor, 0, [[0, 128], [64, 4], [1, 64]]), "c3")
        for i in range(2):
            mset("pool", (vA_t[i][:, :, 0, 64:128], "vA1_%d" % i), 1.0)
            mset("pool", (vA_t[i][:, :, 1, 0:64], "vA1_%d" % i), 1.0)
            mset("pool", (vB_t[i][:, :, 0, 64:128], "vB1_%d" % i), 1.0)
            mset("pool", (vB_t[i][:, :, 1, 0:64], "vB1_%d" % i), 1.0)

        tiles = [(t0, True) for t0 in range(0, NM, TB)] + [(t0, False) for t0 in range(NM, NTOT, TB)]
        nt = len(tiles)

        def loads(ti):
            t0, mine = tiles[ti]
            b = ti % 2
            dma("sp", (h2t[b][:], "h2t%d" % b), h2s[:, :, t0:t0 + TB].rearrange("k p t -> p k t"), "bh%d" % b)
            dma("sp", (rC[b][:], "rC%d" % b), ropeC[t0:t0 + TB, :].rearrange("(s p) c -> p s c", p=128), "brc%d" % b)
            dma("sp", (rS[b][:], "rS%d" % b), ropeS[t0:t0 + TB, :].rearrange("(s p) c -> p s c", p=128), "brs%d" % b)

        STQ = 32
        STK = 128

        loads(0)
        loads(1)
        for ti in range(nt):
            t0, mine = tiles[ti]
            b = ti % 2
            hk = "h2t%d" % b
            if mine:
                tt("dve", (rCq[b][:], "rCq%d" % b), (rC[b][:], "rC%d" % b),
                   (qg_sb[:, 0, :].unsqueeze(1).broadcast_to([128, 4, 64]), "qg_sb"), ALU.mult)
                tt("dve", (rSq[b][:], "rSq%d" % b), (rS[b][:], "rS%d" % b),
                   (qg_sb[:, 1, :].unsqueeze(1).broadcast_to([128, 4, 64]), "qg_sb"), ALU.mult)
            tt("dve", (rCk[b][:], "rCk%d" % b), (rC[b][:], "rC%d" % b),
               (qg_sb[:, 2, :].unsqueeze(1).broadcast_to([128, 4, 64]), "qg_sb"), ALU.mult)
            tt("dve", (rSk[b][:], "rSk%d" % b), (rS[b][:], "rS%d" % b),
               (qg_sb[:, 3, :].unsqueeze(1).broadcast_to([128, 4, 64]), "qg_sb"), ALU.mult)
            for s in range(4):
                p1 = s % 2
                p2 = 2 + (s % 2)
                if mine:
                    for k in range(8):
                        mm((pbank(p1), "P%d" % p1), (h2t[b][:, k, s * 128:(s + 1) * 128], hk),
                           (winb[:, k, 0:512], "winb"), k == 0, k == 7)
                    cp("act", (xq[s][:], "xq%d" % s), (pbank(p1), "P%d" % p1))
                for k in range(8):
                    mm((pbank(p2)[:, 0:384], "P%d" % p2), (h2t[b][:, k, s * 128:(s + 1) * 128], hk),
                       (winb[:, k, 512:896], "winb"), k == 0, k == 7)
                cp("act", (xk[s][:], "xk%d" % s), (pbank(p2)[:, 0:128], "P%d" % p2))
                cp("dve", (vA_t[b][:, s, 0, 0:64], "vA_t%d" % b), (pbank(p2)[:, 128:192], "P%d" % p2))
                cp("dve", (vA_t[b][:, s, 1, 64:128], "vA_t%d" % b), (pbank(p2)[:, 192:256], "P%d" % p2))
                cp("dve", (vB_t[b][:, s, 0, 0:64], "vB_t%d" % b), (pbank(p2)[:, 256:320], "P%d" % p2))
                cp("dve", (vB_t[b][:, s, 1, 64:128], "vB_t%d" % b), (pbank(p2)[:, 320:384], "P%d" % p2))
            jobs = []
            for s in range(4):
                if mine:
                    jobs.append(("q", s, 8, xq[s], "xq%d" % s, STQ + 24 * s, 8))
                jobs.append(("k", s, 2, xk[s], "xk%d" % s, STK + 8 * s, 2))
            for (kind, s, H, xb_, xk_, sc, w) in jobs:
                n = H * 64
                sqb, sqk = (wsq, "wsq") if kind == "q" else (ksq, "ksq")
                tt("dve", (sqb[:, 0:n], sqk), (xb_[:, 0:n], xk_), (xb_[:, 0:n], xk_), ALU.mult)
                red((stat[:, sc:sc + H], "st%d" % sc), (sqb[:, 0:n].rearrange("p (h d) -> p h d", h=H), sqk))
            for (kind, s, H, xb_, xk_, sc, w) in jobs:
                act((stat[:, sc + w:sc + w + H], "stq%d" % sc), (stat[:, sc:sc + H], "st%d" % sc), AF.Sqrt,
                    scale=1.0 / 64, bias=(eps_sb[:], "eps"))
            for jn, (kind, s, H, xb_, xk_, sc, w) in enumerate(jobs):
                n = H * 64
                recip((stat[:, sc + 2 * w:sc + 2 * w + H], "str%d" % sc), (stat[:, sc + w:sc + w + H], "stq%d" % sc))
                if kind == "q":
                    xnb, xnk = wxn[s % 2], "wxn%d" % (s % 2)
                    tb_, tk, ub, uk = wt, "wt", wu, "wu"
                    outb, outk = qr[s], "qr%d" % s
                    Cg, Ck_, Sg, Sk_ = rCq[b], "rCq%d" % b, rSq[b], "rSq%d" % b
                else:
                    xnb, xnk = kxn[s % 2], "kxn%d" % (s % 2)
                    tb_, tk, ub, uk = kt_, "kt_", ku, "ku"
                    outb, outk = kr[s], "kr%d" % s
                    Cg, Ck_, Sg, Sk_ = rCk[b], "rCk%d" % b, rSk[b], "rSk%d" % b
                x3 = xb_[:, 0:n].rearrange("p (h d) -> p h d", h=H)
                rb = stat[:, sc + 2 * w:sc + 2 * w + H].unsqueeze(2).broadcast_to([128, H, 64])
                xn3 = xnb[:, 0:n].rearrange("p (h d) -> p h d", h=H)
                tt("dve", (xn3, xnk), (x3, xk_), (rb, "str%d" % sc), ALU.mult)
                cb = Cg[:, s, :].unsqueeze(1).broadcast_to([128, H, 64])
                t3 = tb_[:, 0:n].rearrange("p (h d) -> p h d", h=H)
                tt("pool", (t3, tk), (xn3, xnk), (cb, Ck_), ALU.mult)
                xn5 = xnb[:, 0:n].rearrange("p (h a w j) -> p h a w j", h=H, a=2, w=2)
                u5 = ub[:, 0:n].rearrange("p (h a w j) -> p h a w j", h=H, a=2, w=2)
                s4 = Sg[:, s, :].rearrange("p (a w j) -> p a w j", a=2, w=2)
                for w_ in range(2):
                    sbc = s4[:, :, w_, :].unsqueeze(1).broadcast_to([128, H, 2, 16])
                    tt("pool", (u5[:, :, :, w_, :], uk), (xn5[:, :, :, 1 - w_, :], xnk), (sbc, Sk_), ALU.mult)
                tt("pool", (outb[:, 0:n], outk), (tb_[:, 0:n], tk), (ub[:, 0:n], uk), ALU.add)
            chunks = list(range(21)) if mine else [4]
            for n_, c in enumerate(chunks):
                pf = 4 + (n_ % 2)
                for k in range(8):
                    mm((pbank(pf), "P%d" % pf), (winb[:, k, 896 + c * 128:896 + (c + 1) * 128], "winb"),
                       (h2t[b][:, k, :], hk), k == 0, k == 7)
                if c < 4:
                    cp("act", (qbT_t[b][:, c, :], "qbT_t%d" % b), (pbank(pf), "P%d" % pf))
                elif c == 4:
                    cp("act", (kbT_t[b][:], "kbT_t%d" % b), (pbank(pf), "P%d" % pf))
                else:
                    act((sg_t[b][:, c - 5, :], "sg_t%d" % b), (pbank(pf), "P%d" % pf), AF.Sigmoid)
            for s in range(4):
                tb = 6 + (s % 2)
                pvT = pbank_bf(tb)
                if mine:
                    for i in range(4):
                        tr((pvT[:, i * 128:(i + 1) * 128], "P%d" % tb), (qr[s][:, i * 128:(i + 1) * 128], "qr%d" % s))
                tr((pvT[:, 512:640], "P%d" % tb), (kr[s][:], "kr%d" % s))
                if mine:
                    cp("dve", (qaT_t[b][:, :, s * 128:(s + 1) * 128], "qaT_t%d" % b),
                       (pvT[:, 0:512].rearrange("p (i t) -> p i t", i=4), "P%d" % tb))
                cp("dve", (kaT_t[b][:, s * 128:(s + 1) * 128], "kaT_t%d" % b), (pvT[:, 512:640], "P%d" % tb))
            dma("pool", kaTs[:, t0:t0 + TB], (kaT_t[b][:], "kaT_t%d" % b), "bska%d" % b)
            dma("pool", kbTs[:, t0:t0 + TB], (kbT_t[b][:], "kbT_t%d" % b), "bskb%d" % b)
            P.add("pool", (lambda e, o_=vaAs[t0:t0 + TB].rearrange("(s p) h d -> p s h d", p=128), i_=vA_t[b][:]:
                           e.dma_start(out=o_, in_=i_)), reads=["vA_t%d" % b, "vA1_%d" % b], writes=[],
                  dma_key="bsva%d" % b)
            P.add("pool", (lambda e, o_=vaBs[t0:t0 + TB].rearrange("(s p) h d -> p s h d", p=128), i_=vB_t[b][:]:
                           e.dma_start(out=o_, in_=i_)), reads=["vB_t%d" % b, "vB1_%d" % b], writes=[],
                  dma_key="bsvb%d" % b)
            if mine:
                dma("pool", qaTs[:, :, t0:t0 + TB].rearrange("i p t -> p i t"), (qaT_t[b][:], "qaT_t%d" % b), "bsqa%d" % b)
                dma("pool", qbTs[:, :, t0:t0 + TB].rearrange("i p t -> p i t"), (qbT_t[b][:], "qbT_t%d" % b), "bsqb%d" % b)
                dma("pool", sgs[:, :, t0:t0 + TB].rearrange("c p t -> p c t"), (sg_t[b][:], "sg_t%d" % b), "bssg%d" % b)
            if ti + 2 < nt:
                loads(ti + 2)
        P.barrier()

    def phase_c():
        phase_reset()
        TC = 512
        wbrA = sb("wbrA", [128, 4, D], BF16)
        wbrB = sb("wbrB", [128, 4, D], BF16)
        woutb = sb("woutb", [128, 8, D], BF16)
        kaTz = [sb("kaTz%d" % e, [128, SEQ_P], BF16) for e in range(2)]
        vaugA = sb("vaugA", [128, SEQ_P // 128, 2, 128], BF16)
        es_bc = sb("es_bc", [128, 8], F32)
        valid_sb = sb("valid_sb", [128, 2], F32)
        qaT_t = [sb("cqa%d" % i, [128, 4, TC], BF16) for i in range(2)]
        qbT_t = [sb("cqb%d" % i, [128, 4, TC], BF16) for i in range(2)]
        kbwz = [[sb("kbwz%d_%d" % (i, e), [128, 768], BF16) for e in range(2)] for i in range(2)]
        vbw = [sb("vbw%d" % i, [128, 6, 2, 128], BF16) for i in range(2)]
        x1t = sb("x1t", [128, 4, D], F32)
        sg_t = sb("csg", [128, 16, TC], BF16)
        pT = [sb("pT%d" % i, [128, 1024], BF16) for i in range(3)]
        pTB = [sb("pTB%d" % i, [128, 384], BF16) for i in range(2)]
        oTA = sb("oTA", [128, 4, TC], BF16)
        oTB = sb("oTB", [128, 4, TC], BF16)
        mT = sb("mT", [128, 8, TC], BF16)
        rden = [sb("rden%d" % i, [128, 512], F32) for i in range(2)]
        tmpd = sb("tmpd", [128, 512], F32)
        tmA = [sb("tmA%d" % i, [128, 512], F32) for i in range(1)]
        tmB = [sb("tmB%d" % i, [128, 512], F32) for i in range(1)]
        for (wt_, src, key) in ((wbrA, wba, "wbrA"), (wbrB, wbb, "wbrB")):
            sv = src.rearrange("(e i p) d -> e p i d", e=2, i=4, p=64)
            for e in range(2):
                dma("pool", (wt_[e * 64:(e + 1) * 64, :, :], key), sv[e], "cw" + key)
        dma("pool", (woutb[:], "woutb"), wout.rearrange("(k p) d -> p k d", p=128), "cwo")
        dma("sp", (valid_sb[:], "valid"), validd, "c4")
        dma("sp", (es_bc[:], "es_raw"), sink.partition_broadcast(128), "c5")
        act((es_bc[:], "es"), (es_bc[:], "es_raw"), AF.Exp)
        mset("pool", (kaTz[0][64:128, :], "kaT"), 0.0)
        mset("pool", (kaTz[1][0:64, :], "kaT"), 0.0)
        for i in range(2):
            mset("pool", (kbwz[i][0][64:128, :], "kbw%d" % i), 0.0)
            mset("pool", (kbwz[i][1][0:64, :], "kbw%d" % i), 0.0)
        seqs = [(0, HP, NM, SEQ_P), (HP, HS, NM + HP, SEQ_S)]
        tile_list = []
        for si, (ms, ml, ps_, sl) in enumerate(seqs):
            for t in range(ml // TC):
                tile_list.append((si, t))
        nt = len(tile_list)

        def kb_load(b, w0, w1, a0):
            n = (w1 - w0) * 128
            dma("sp", (kbwz[b][0][0:64, w0 * 128:w1 * 128], "kbw%d" % b), kbTs[0:64, a0:a0 + n], "clk%d" % b)
            dma("sp", (kbwz[b][1][64:128, w0 * 128:w1 * 128], "kbw%d" % b), kbTs[64:128, a0:a0 + n], "clk%d" % b)
            dma("sp", (vbw[b][:, w0:w1, :, :], "vbw%d" % b), vaBs[a0:a0 + n].rearrange("(c p) h d -> p c h d", p=128),
                "clv%d" % b)

        def tile_loads(ti):
            si, t = tile_list[ti]
            ms, ml, ps_, sl = seqs[si]
            b = ti % 2
            q0 = ms + t * TC
            dma("sp", (qaT_t[b][:], "cqa%d" % b), qaTs[:, :, q0:q0 + TC].rearrange("i p t -> p i t"), "cla%d" % b)
            dma("sp", (qbT_t[b][:], "cqb%d" % b), qbTs[:, :, q0:q0 + TC].rearrange("i p t -> p i t"), "clb%d" % b)
            ntl = ml // TC
            lo = 0 if t > 0 else 1
            hi = 6 if t < ntl - 1 else 5
            kb_load(b, lo, hi, q0 - 128 + lo * 128)
            if t == 0:
                kb_load(b, 0, 1, ps_ + ml - 128)
                ts("dve", (vbw[b][:, 0, :, :], "vbw%d" % b), (vbw[b][:, 0, :, :], "vbw%d" % b),
                   (valid_sb[:, 0:1], "valid"), None, ALU.mult)
            if t == ntl - 1:
                kb_load(b, 5, 6, ps_)
                ts("dve", (vbw[b][:, 5, :, :], "vbw%d" % b), (vbw[b][:, 5, :, :], "vbw%d" % b),
                   (valid_sb[:, 1:2], "valid"), None, ALU.mult)

        def seq_loads(si):
            ms, ml, ps_, sl = seqs[si]
            for (src0, dst0) in ((ms, 0), (ps_, ml)):
                for c0 in range(0, ml, 1024):
                    d0 = dst0 + c0
                    dma("sp", (kaTz[0][0:64, d0:d0 + 1024], "kaT"), kaTs[0:64, src0 + c0:src0 + c0 + 1024], "clka")
                    dma("sp", (kaTz[1][64:128, d0:d0 + 1024], "kaT"), kaTs[64:128, src0 + c0:src0 + c0 + 1024], "clka")
                    dma("sp", (vaugA[:, d0 // 128:d0 // 128 + 8, :, :], "vaugA"),
                        vaAs[src0 + c0:src0 + c0 + 1024].rearrange("(c p) h d -> p c h d", p=128), "clva")

        def normalize(psb, pkey, e, dst, dkey, es_col):
            nlo, dlo = (0, 64) if e == 0 else (64, 0)
            rb = e
            den = (psb[dlo:dlo + 64, :], pkey)
            if es_col is not None:
                ts("dve", (tmpd[dlo:dlo + 64, :], "tmpd"), den, (es_bc[dlo:dlo + 64, es_col:es_col + 1], "es"), None, ALU.add)
                den = (tmpd[dlo:dlo + 64, :], "tmpd")
            recip((rden[rb][nlo:nlo + 64, :], "rden%d" % rb), den)
            tt("dve", (dst[nlo:nlo + 64, :], dkey), (psb[nlo:nlo + 64, :], pkey), (rden[rb][nlo:nlo + 64, :], "rden%d" % rb),
               ALU.mult)

        seq_loads(0)
        tile_loads(0)
        for ti in range(nt):
            si, t = tile_list[ti]
            ms, ml, ps_, sl = seqs[si]
            b = ti % 2
            q0 = ms + t * TC
            nch = sl // 128
            if ti + 1 < nt and tile_list[ti + 1][0] == si:
                tile_loads(ti + 1)
            dma("sp", (sg_t[:], "csg"), sgs[:, :, q0:q0 + TC].rearrange("c p t -> p c t"), "clsg")
            dma("sp", (x1t[:], "x1t"), x1s[q0:q0 + TC, :].rearrange("(s p) d -> p s d", p=128), "clx1")
            qa_k = "cqa%d" % b
            qb_k = "cqb%d" % b
            heads = [(i, e) for i in range(4) for e in range(2)]
            stepsB = [(i, e, qb) for (i, e) in heads for qb in range(4)]

            def SB(n):
                i, e, qb = stepsB[n]
                h = i + 4 * e
                bank = n % 2
                v = pbank(bank)[:, 0:384].rearrange("p (o q) -> p o q", o=3)
                for o in range(3):
                    mm((v[:, o, :], "P%d" % bank), (kbwz[b][e][:, (qb + o) * 128:(qb + o + 1) * 128], "kbw%d" % b),
                       (qbT_t[b][:, i, qb * 128:(qb + 1) * 128], qb_k), True, False)
                    mm((v[:, o, :], "P%d" % bank), (ident[:], "ident"), (Bt[:, o, h, :], "Bt"), False, True)
                act((pTB[n % 2][:], "pTB%d" % (n % 2)), (pbank(bank)[:, 0:384], "P%d" % bank), AF.Exp, scale=0.125)

            def PVB(n):
                i, e, qb = stepsB[n]
                h = i + 4 * e
                hidx = heads.index((i, e))
                ob = 2 + (hidx % 2)
                pv = pTB[n % 2][:].rearrange("p (o q) -> p o q", o=3)
                for o in range(3):
                    mm((pbank(ob)[:, qb * 128:(qb + 1) * 128], "P%d" % ob), (vbw[b][:, qb + o, e, :], "vbw%d" % b),
                       (pv[:, o, :], "pTB%d" % (n % 2)), o == 0, o == 2)
                if qb == 3:
                    normalize(pbank(ob), "P%d" % ob, e, oTB[:, i, :], "oTB", h)

            SB(0)
            for n in range(len(stepsB)):
                if n + 1 < len(stepsB):
                    SB(n + 1)
                PVB(n)
            ncp = nch // 2
            stepsA = [(i, e, c2) for (i, e) in heads for c2 in range(ncp)]
            NA = len(stepsA)

            def SA(n):
                i, e, c2 = stepsA[n]
                g = n % 2
                for hh in range(2):
                    c = 2 * c2 + hh
                    bank = 2 * g + hh
                    mm((pbank(bank), "P%d" % bank), (kaTz[e][:, c * 128:(c + 1) * 128], "kaT"),
                       (qaT_t[b][:, i, :], qa_k), True, True)
                p_ = n % 3
                P.add("act", (lambda e_, o_=pT[p_][:], i_=PP[g][:]: e_.activation(out=o_, in_=i_, func=AF.Exp, scale=0.125)),
                      reads=["P%d" % (2 * g), "P%d" % (2 * g + 1)], writes=["pT%d" % p_])

            def PVA(n):
                i, e, c2 = stepsA[n]
                hidx = heads.index((i, e))
                ob = 4 + (hidx % 2)
                p_ = n % 3
                for hh in range(2):
                    c = 2 * c2 + hh
                    mm((pbank(ob), "P%d" % ob), (vaugA[:, c, e, :], "vaugA"), (pT[p_][:, hh * 512:(hh + 1) * 512], "pT%d" % p_),
                       c == 0, c == nch - 1)
                if c2 == ncp - 1:
                    normalize(pbank(ob), "P%d" % ob, e, oTA[:, i, :], "oTA", None)

            SA(0)
            for n in range(NA):
                if n + 1 < NA:
                    SA(n + 1)
                PVA(n)
            for dc in range(8):
                pya = 6 + (dc % 2)
                pyb = (dc % 2)
                for i in range(4):
                    mm((pbank(pya), "P%d" % pya), (wbrA[:, i, dc * 128:(dc + 1) * 128], "wbrA"), (oTA[:, i, :], "oTA"),
                       i == 0, i == 3)
                for i in range(4):
                    mm((pbank(pyb), "P%d" % pyb), (wbrB[:, i, dc * 128:(dc + 1) * 128], "wbrB"), (oTB[:, i, :], "oTB"),
                       i == 0, i == 3)
                x_ = 0
                tt("dve", (tmA[x_][:], "tmA%d" % x_), (pbank(pya), "P%d" % pya), (sg_t[:, dc, :], "csg"), ALU.mult)
                tt("dve", (tmB[x_][:], "tmB%d" % x_), (pbank(pyb), "P%d" % pyb), (sg_t[:, 8 + dc, :], "csg"), ALU.mult)
                tt("pool", (mT[:, dc, :], "mT"), (tmA[x_][:], "tmA%d" % x_), (tmB[x_][:], "tmB%d" % x_), ALU.add)
            for s in range(4):
                for hf in range(2):
                    pw = 2 + ((2 * s + hf) % 2)
                    for dc in range(8):
                        mm((pbank(pw), "P%d" % pw), (mT[:, dc, s * 128:(s + 1) * 128], "mT"),
                           (woutb[:, dc, hf * 512:(hf + 1) * 512], "woutb"), dc == 0, dc == 7)
                    tt("dve", (x1t[:, s, hf * 512:(hf + 1) * 512], "x1t"), (pbank(pw), "P%d" % pw),
                       (x1t[:, s, hf * 512:(hf + 1) * 512], "x1t"), ALU.add)
            dma("pool", x2s[q0:q0 + TC, :].rearrange("(s p) d -> p s d", p=128), (x1t[:], "x1t"), "csx2")
            if ti + 1 < nt and tile_list[ti + 1][0] != si:
                seq_loads(tile_list[ti + 1][0])
                tile_loads(ti + 1)
        P.barrier()

    if "A" in phases:
        ffn_phase("A", [(xm, NM, True, 0), (xp, NM, False, NM)], w1, w2, 0, "A")
    if "B" in phases:
        phase_b()
    if "C" in phases:
        phase_c()
    if "D" in phases:
        ffn_phase("D", [(x2s, NM, True, 0)], w3, w4, 2, "D")
    if not final_ops:
        final_ops.append([o for o in P.ops if o.is_dma][-1])
    P.emit(final_ops)
    return nc, P


def _prep_inputs(inputs):
    f32 = np.float32
    g = {k: np.asarray(v) for k, v in inputs.items()}
    oh_list, oh_tiles, mask_tiles = _get_bias_structure()
    w_in = g["w_in"][0]
    qa_cols = np.concatenate([np.arange(h * 64, (h + 1) * 64) for h in QPERM])
    ka_cols = np.arange(512, 640)
    va_cols = np.arange(640, 768)
    qb_cols = 768 + qa_cols
    kb_cols = np.arange(1280, 1408)
    vb_cols = np.arange(1408, 1536)
    ga_cols = np.arange(1536, 2560)
    gb_cols = np.arange(2560, 3584)
    perm = np.concatenate([qa_cols, ka_cols, va_cols, vb_cols, qb_cols, kb_cols, ga_cols, gb_cols])
    win_p = np.ascontiguousarray(w_in[:, perm])
    gc = np.stack([g["norm_ffn1"][0], g["norm_mix"][0], g["norm_ffn2"][0]], 0)
    gcols = np.ascontiguousarray(gc.reshape(3, 8, 128).transpose(2, 0, 1)).astype(f32)

    def swap(v):
        v4 = v.reshape(2, 2, 16)
        return np.ascontiguousarray(v4[:, ::-1, :]).reshape(64)

    qgv = g["q_norm_a"][0].astype(f32)
    kgv = g["k_norm_a"][0].astype(f32)
    qg = np.stack([qgv, swap(qgv), kgv, swap(kgv)], 0).astype(f32)
    common = {
        "w1": np.ascontiguousarray(g["w_ffn1_in"][0]), "w2": np.ascontiguousarray(g["w_ffn1_out"][0]),
        "w3": np.ascontiguousarray(g["w_ffn2_in"][0]), "w4": np.ascontiguousarray(g["w_ffn2_out"][0]),
        "win": win_p, "wba": np.ascontiguousarray(g["w_branch_a"][0]), "wbb": np.ascontiguousarray(g["w_branch_b"][0]),
        "wout": np.ascontiguousarray(g["w_out"][0]), "gcols": gcols, "gfin": np.ascontiguousarray(g["norm_final"]).astype(f32),
        "qg": qg, "sink": np.ascontiguousarray(g["sink_b"][0]).astype(f32),
        "relb": np.ascontiguousarray(g["rel_bias"].reshape(256)).astype(f32),
        "identd": np.eye(128, dtype=f32), "ohd": oh_tiles.astype(f32), "maskd": mask_tiles.astype(f32),
    }
    in_maps = []
    for c in range(8):
        p, r = c // 2, c % 2
        xp_ = g["x_prompt"][p]
        xs_ = g["x_sample"][p]
        mine = np.concatenate([xp_[r * HP:(r + 1) * HP], xs_[r * HS:(r + 1) * HS]], 0)
        part = np.concatenate([xp_[(1 - r) * HP:(2 - r) * HP], xs_[(1 - r) * HS:(2 - r) * HS]], 0)
        pos = np.concatenate([np.arange(r * HP, (r + 1) * HP), np.arange(r * HS, (r + 1) * HS),
                              np.arange((1 - r) * HP, (2 - r) * HP), np.arange((1 - r) * HS, (2 - r) * HS)])
        CC, SS = _rope_tables(pos)
        valid = np.zeros((128, 2), f32)
        valid[:, 0] = 1.0 if r == 1 else 0.0
        valid[:, 1] = 1.0 if r == 0 else 0.0
        m = dict(common)
        m.update({"xm": np.ascontiguousarray(mine, dtype=f32), "xp": np.ascontiguousarray(part, dtype=f32),
                  "ropeC": CC, "ropeS": SS, "validd": valid})
        in_maps.append(m)
    return in_maps


_CACHE = {}


def kernel(**inputs):
    in_maps = _prep_inputs(inputs)
    if "nc" not in _CACHE:
        _CACHE["nc"] = build_program()[0]
    nc = _CACHE["nc"]
    res = run_bass_kernel_spmd(nc, in_maps, core_ids=list(range(8)))
    yp = np.zeros((4, SEQ_P, D), np.float32)
    ys = np.zeros((4, SEQ_S, D), np.float32)
    for c in range(8):
        p, r = c // 2, c % 2
        y = np.asarray(res.results[c]["y"])
        yp[p, r * HP:(r + 1) * HP] = y[0:HP]
        ys[p, r * HS:(r + 1) * HS] = y[HP:NM]
    return (yp, ys)
```

```python
import math
import os
import numpy as np
import ml_dtypes
import concourse.bass as bass
import concourse.mybir as mybir
from concourse.bass_utils import run_bass_kernel_spmd

F32 = mybir.dt.float32
BF16 = mybir.dt.bfloat16
AF = mybir.ActivationFunctionType
ALU = mybir.AluOpType
AX = mybir.AxisListType

D = 1024
DFF = 2816
NJ = DFF // 128
SEQ_P = 8192
SEQ_S = 4096
HP = SEQ_P // 2
HS = SEQ_S // 2
NM = HP + HS
NTOT = 2 * NM
TA = 384
EPS = 1e-6
QPERM = [0, 4, 1, 5, 2, 6, 3, 7]
NEG = -30000.0

DEBUG = os.environ.get("MK_DEBUG", "") != ""


class Op:
    __slots__ = ("stream", "fn", "is_dma", "key", "deps", "signal", "val", "sem", "idx")


class Prog:
    COMPUTE = ("pe", "act", "dve", "pool")

    def __init__(self, nc):
        self.nc = nc
        self.ops = []
        self.lastw = {}
        self.readers = {}
        self.dma_cnt = {}
        self.bar_start = 0
        self.bar_deps = []
        self.bar_pending = set()

    def barrier(self):
        deps = {}
        for op in self.ops[self.bar_start:]:
            if op.is_dma:
                deps[("d", op.key)] = op
            else:
                deps[op.stream] = op
        for o in self.bar_deps:
            k = ("d", o.key) if o.is_dma else o.stream
            deps.setdefault(k, o)
        self.bar_deps = list(deps.values())
        for o in self.bar_deps:
            o.signal = True
        self.bar_start = len(self.ops)
        self.bar_pending = {"pe", "act", "dve", "pool", "sp"}

    def add(self, stream, fn, reads=(), writes=(), dma_key=None):
        op = Op()
        op.stream = stream
        op.fn = fn
        op.is_dma = dma_key is not None
        op.key = dma_key
        op.signal = op.is_dma
        op.val = 0
        op.sem = None
        op.idx = len(self.ops)
        deps = []
        for r in reads:
            w = self.lastw.get(r)
            if w is not None:
                deps.append((w, "raw"))
            if isinstance(r, str) and r[0] == "P" and r[1:].isdigit():
                rd = self.readers.get(r)
                if rd:
                    for k, o in rd.items():
                        if k != "dma" and k != stream:
                            deps.append((o, "war"))
        for r in writes:
            w = self.lastw.get(r)
            if w is not None:
                deps.append((w, "waw"))
            rd = self.readers.get(r)
            if rd:
                for k, o in rd.items():
                    if k == "dma":
                        for oo in o:
                            deps.append((oo, "war"))
                    else:
                        deps.append((o, "war"))
        keep = []
        seen = set()
        for (o, kind) in deps:
            if o is op or id(o) in seen:
                continue
            if (not o.is_dma) and (not op.is_dma) and o.stream == op.stream:
                if kind != "raw" or op.stream == "pe":
                    continue
            seen.add(id(o))
            o.signal = True
            keep.append(o)
        if stream in self.bar_pending:
            self.bar_pending.discard(stream)
            for o in self.bar_deps:
                if id(o) in seen:
                    continue
                if (not o.is_dma) and (not op.is_dma) and o.stream == op.stream == "pe":
                    continue
                seen.add(id(o))
                keep.append(o)
        op.deps = keep
        for r in reads:
            rd = self.readers.setdefault(r, {})
            if op.is_dma:
                rd.setdefault("dma", []).append(op)
            else:
                rd[op.stream] = op
        for r in writes:
            self.lastw[r] = op
            self.readers[r] = {}
        if op.is_dma:
            c = self.dma_cnt.get(dma_key, 0) + 1
            self.dma_cnt[dma_key] = c
            op.val = 16 * c
        self.ops.append(op)
        return op

    def emit(self, final_waits):
        nc = self.nc
        sems = {}

        def get_sem(name):
            if name not in sems:
                sems[name] = nc.alloc_semaphore("s_" + name.replace("/", "_"))
            return sems[name]

        LIM = 30000
        cnt = {s: 0 for s in self.COMPUTE}
        for op in self.ops:
            if op.is_dma:
                op.sem = get_sem("d_" + str(op.key))
            elif op.signal:
                c = cnt[op.stream]
                op.sem = get_sem("%s%d" % (op.stream, c // LIM))
                op.val = c % LIM + 1
                cnt[op.stream] = c + 1
        self.nsems = len(sems)
        streams = {}
        for op in self.ops:
            streams.setdefault(op.stream, []).append(op)
        fin = list(final_waits)

        def run_stream(e, ops, is_last_stream):
            waited = {}
            for op in ops:
                for d in op.deps:
                    k = id(d.sem)
                    if waited.get(k, 0) >= d.val:
                        continue
                    e.wait_ge(d.sem, d.val)
                    waited[k] = d.val
                ins = op.fn(e)
                if op.signal:
                    ins.then_inc(op.sem, 16 if op.is_dma else 1)
            if is_last_stream:
                for o in fin:
                    e.wait_ge(o.sem, o.val)

        with nc.Block() as block:
            if "sp" in streams:
                @block.sync
                def _(e):
                    run_stream(e, streams["sp"], True)
            if "pool" in streams:
                @block.gpsimd
                def _(e):
                    run_stream(e, streams["pool"], False)
            if "act" in streams:
                @block.scalar
                def _(e):
                    run_stream(e, streams["act"], False)
            if "dve" in streams:
                @block.vector
                def _(e):
                    run_stream(e, streams["dve"], False)
            if "pe" in streams:
                @block.tensor
                def _(e):
                    run_stream(e, streams["pe"], False)


def _rw(args):
    return [a[1] for a in args if a is not None and isinstance(a, tuple)]


def _ap(a):
    return a[0] if isinstance(a, tuple) else a


def _t5_bucket_np(rel):
    try:
        import jax
        import jax.numpy as jnp
        with jax.default_device(jax.devices("cpu")[0]):
            r = jnp.asarray(rel, dtype=jnp.int32)
            nb = 16
            max_exact = 8
            ret = jnp.where(r > 0, nb, 0)
            n = jnp.abs(r)
            large = max_exact + (jnp.log(jnp.maximum(n, 1).astype(jnp.float32) / max_exact)
                                 / math.log(128 / max_exact) * (nb - max_exact)).astype(jnp.int32)
            large = jnp.minimum(large, nb - 1)
            out = ret + jnp.where(n < max_exact, n, large)
            return np.asarray(out)
    except Exception:
        rel = np.asarray(rel, dtype=np.int64)
        nb, max_exact = 16, 8
        ret = np.where(rel > 0, nb, 0)
        n = np.abs(rel)
        large = max_exact + (np.log(np.maximum(n, 1).astype(np.float32) / np.float32(max_exact))
                             / np.float32(math.log(128 / max_exact)) * np.float32(nb - max_exact)).astype(np.int32)
        large = np.minimum(large, nb - 1)
        return ret + np.where(n < max_exact, n, large)


def _bias_structure():
    kk = np.arange(128)[:, None]
    qq = np.arange(128)[None, :]
    lst = []
    tiles = []
    masks = []
    for o in range(3):
        rel = (o - 1) * 128 + kk - qq
        bk = _t5_bucket_np(rel)
        valid = np.abs(rel) <= 128
        for b in range(32):
            m = (bk == b) & valid
            if m.any():
                lst.append((o, b))
                tiles.append(m.astype(np.float32) * 8.0)
        if o != 1:
            masks.append(np.where(valid, 0.0, NEG).astype(np.float32))
    return lst, np.stack(tiles), np.stack(masks)


_OH_LIST, _OH_TILES, _MASK_TILES = None, None, None


def _get_bias_structure():
    global _OH_LIST, _OH_TILES, _MASK_TILES
    if _OH_LIST is None:
        _OH_LIST, _OH_TILES, _MASK_TILES = _bias_structure()
    return _OH_LIST, _OH_TILES, _MASK_TILES


def _rope_tables(pos):
    pos = np.asarray(pos, dtype=np.int64)
    row = (pos // 64).astype(np.float32)
    col = (pos % 64).astype(np.float32)
    inv = (np.float32(10000.0) ** (-np.arange(0, 32, 2, dtype=np.float32) / np.float32(32))).astype(np.float32)
    CC = np.zeros((len(pos), 64), np.float32)
    SS = np.zeros((len(pos), 64), np.float32)
    for a, p in enumerate((row, col)):
        ang = (p[:, None] * inv[None, :]).astype(np.float32)
        c = np.cos(ang).astype(np.float32)
        s = np.sin(ang).astype(np.float32)
        CC[:, a * 32:a * 32 + 16] = c
        CC[:, a * 32 + 16:a * 32 + 32] = c
        SS[:, a * 32:a * 32 + 16] = -s
        SS[:, a * 32 + 16:a * 32 + 32] = s
    return CC, SS


SB_BASE = 16512 + 64
SB_LIMIT = 228480


def build_program(phases="ABCD"):
    oh_list, _, _ = _get_bias_structure()
    NOH = len(oh_list)
    nc = bass.Bass("TRN2", target_bir_lowering=False)
    P = Prog(nc)

    def din(name, shape, dt=F32):
        return nc.dram_tensor(name, list(shape), dt, kind="ExternalInput").ap()

    def dscr(name, shape, dt):
        kind = "ExternalOutput" if DEBUG else "Internal"
        return nc.dram_tensor(name, list(shape), dt, kind=kind).ap()

    xm = din("xm", [NM, D])
    xp = din("xp", [NM, D])
    w1 = din("w1", [D, 2 * DFF])
    w2 = din("w2", [DFF, D])
    w3 = din("w3", [D, 2 * DFF])
    w4 = din("w4", [DFF, D])
    win = din("win", [D, 3584])
    wba = din("wba", [512, D])
    wbb = din("wbb", [512, D])
    wout = din("wout", [D, D])
    gcols = din("gcols", [128, 3, 8])
    gfin = din("gfin", [D])
    qg = din("qg", [4, 64])
    sink = din("sink", [8])
    relb = din("relb", [256])
    ropeC = din("ropeC", [NTOT, 64])
    ropeS = din("ropeS", [NTOT, 64])
    identd = din("identd", [128, 128])
    ohd = din("ohd", [NOH, 128, 128])
    maskd = din("maskd", [2, 128, 128])
    validd = din("validd", [128, 2])
    yout = nc.dram_tensor("y", [NM, D], F32, kind="ExternalOutput").ap()

    x1s = dscr("x1s", [NM, D], F32)
    h2s = dscr("h2s", [8, 128, NTOT], BF16)
    kaTs = dscr("kaTs", [128, NTOT], BF16)
    kbTs = dscr("kbTs", [128, NTOT], BF16)
    vaAs = dscr("vaAs", [NTOT, 2, 128], BF16)
    vaBs = dscr("vaBs", [NTOT, 2, 128], BF16)
    qaTs = dscr("qaTs", [4, 128, NM], BF16)
    qbTs = dscr("qbTs", [4, 128, NM], BF16)
    sgs = dscr("sgs", [16, 128, NM], BF16)
    x2s = dscr("x2s", [NM, D], F32)

    PP = [nc.alloc_psum_tensor("pp%d" % i, [128, 1024], F32) for i in range(4)]

    def pbank(i):
        return PP[i // 2][:, (i % 2) * 512:(i % 2 + 1) * 512]

    def pbank_bf(i):
        return PP[i // 2][:].bitcast(BF16)[:, (i % 2) * 1024:(i % 2 + 1) * 1024]

    arena = {"pers": SB_BASE, "cur": SB_BASE, "n": 0}

    def sb(name, shape, dt, persistent=False):
        nbytes = int(np.prod(shape[1:])) * (4 if dt == F32 else 2)
        nbytes = (nbytes + 63) // 64 * 64
        off = arena["cur"]
        assert off + nbytes <= SB_LIMIT, ("SBUF overflow", name, off, nbytes)
        arena["cur"] = off + nbytes
        arena["n"] += 1
        h = nc.alloc_sbuf_tensor_at("%s_%d" % (name, arena["n"]), list(shape), dt, offset=off)
        if persistent:
            arena["pers"] = arena["cur"]
        return h

    def phase_reset():
        arena["cur"] = arena["pers"]

    top = {"cur": SB_LIMIT - 1152}

    def sb_top(name, shape, dt):
        nbytes = int(np.prod(shape[1:])) * (4 if dt == F32 else 2)
        nbytes = (nbytes + 63) // 64 * 64
        top["cur"] -= nbytes
        arena["n"] += 1
        return nc.alloc_sbuf_tensor_at("%s_%d" % (name, arena["n"]), list(shape), dt, offset=top["cur"])

    ident = sb("ident", [128, 128], BF16, True)
    gcol_sb = sb("gcol_sb", [128, 3, 8], F32, True)
    eps_sb = sb("eps_sb", [128, 1], F32, True)
    junk = sb("junk", [128, 1024], BF16, True)
    stat = sb("stat", [128, 256], F32, True)

    def dma(stream, out, in_, key):
        o, i = _ap(out), _ap(in_)
        return P.add(stream, lambda e: e.dma_start(out=o, in_=i), reads=_rw([in_]), writes=_rw([out]),
                     dma_key=key)

    def mm(out, lhsT, rhs, start, stop):
        o, l, r = _ap(out), _ap(lhsT), _ap(rhs)
        return P.add("pe", lambda e: e.matmul(o, lhsT=l, rhs=r, start=start, stop=stop),
                     reads=_rw([lhsT, rhs]), writes=_rw([out]))

    def tr(out, in_):
        o, i = _ap(out), _ap(in_)
        idn = ident[:]
        return P.add("pe", lambda e: e.transpose(o, i, idn), reads=_rw([in_]) + ["ident"], writes=_rw([out]))

    def act(out, in_, func, scale=1.0, bias=None, accum=None):
        o, i = _ap(out), _ap(in_)
        sc = _ap(scale) if isinstance(scale, tuple) else scale
        kw = {}
        if bias is not None:
            kw["bias"] = _ap(bias)
        if accum is not None:
            kw["accum_out"] = _ap(accum)
        rd = _rw([in_]) + (_rw([scale]) if isinstance(scale, tuple) else []) + (_rw([bias]) if bias is not None else [])
        wr = _rw([out]) + (_rw([accum]) if accum is not None else [])
        return P.add("act", lambda e: e.activation(out=o, in_=i, func=func, scale=sc, **kw), reads=rd, writes=wr)

    def tt(eng, out, in0, in1, op):
        o, a, b = _ap(out), _ap(in0), _ap(in1)
        return P.add(eng, lambda e: e.tensor_tensor(out=o, in0=a, in1=b, op=op), reads=_rw([in0, in1]),
                     writes=_rw([out]))

    def ts(eng, out, in0, s1, s2, op0, op1=None):
        o, a = _ap(out), _ap(in0)
        s1a = _ap(s1) if isinstance(s1, tuple) else s1
        s2a = _ap(s2) if isinstance(s2, tuple) else s2
        rd = _rw([in0]) + (_rw([s1]) if isinstance(s1, tuple) else []) + (_rw([s2]) if isinstance(s2, tuple) else [])
        if op1 is None:
            return P.add(eng, lambda e: e.tensor_scalar(out=o, in0=a, scalar1=s1a, scalar2=None, op0=op0),
                         reads=rd, writes=_rw([out]))
        return P.add(eng, lambda e: e.tensor_scalar(out=o, in0=a, scalar1=s1a, scalar2=s2a, op0=op0, op1=op1),
                     reads=rd, writes=_rw([out]))

    def stt(eng, out, in0, scalar, in1, op0, op1):
        o, a, b = _ap(out), _ap(in0), _ap(in1)
        s = _ap(scalar) if isinstance(scalar, tuple) else scalar
        rd = _rw([in0, in1]) + (_rw([scalar]) if isinstance(scalar, tuple) else [])
        return P.add(eng, lambda e: e.scalar_tensor_tensor(out=o, in0=a, scalar=s, in1=b, op0=op0, op1=op1),
                     reads=rd, writes=_rw([out]))

    def cp(eng, out, in_):
        o, i = _ap(out), _ap(in_)
        if eng == "act":
            return P.add("act", lambda e: e.copy(out=o, in_=i), reads=_rw([in_]), writes=_rw([out]))
        return P.add(eng, lambda e: e.tensor_copy(out=o, in_=i), reads=_rw([in_]), writes=_rw([out]))

    def recip(out, in_):
        o, i = _ap(out), _ap(in_)
        return P.add("dve", lambda e: e.reciprocal(out=o, in_=i), reads=_rw([in_]), writes=_rw([out]))

    def red(out, in_):
        o, i = _ap(out), _ap(in_)
        return P.add("dve", lambda e: e.tensor_reduce(out=o, in_=i, axis=AX.X, op=ALU.add), reads=_rw([in_]),
                     writes=_rw([out]))

    def mset(eng, out, val):
        o = _ap(out)
        return P.add(eng, lambda e: e.memset(o, val), reads=[], writes=_rw([out]))

    final_ops = []

    dma("pool", (ident[:], "ident"), identd, "c0")
    dma("sp", (gcol_sb[:], "gcol"), gcols, "c1")
    mset("dve", (eps_sb[:], "eps"), EPS)

    Bt = sb("Bt", [128, 3, 8, 128], BF16, True)
    Bacc = sb_top("Bacc", [128, 8, 128], F32)
    rb_bc = sb_top("rb_bc", [128, 256], F32)
    ohb = [sb_top("ohb%d" % i, [128, 128], BF16) for i in range(2)]
    mskb = sb_top("mskb", [128, 2, 128], F32)
    dma("sp", (rb_bc[:], "rb_bc"), relb.partition_broadcast(128), "c6")
    dma("sp", (mskb[:], "mskb"), maskd.rearrange("m p q -> p m q"), "c7")
    n_ = 0
    for o in range(3):
        mset("dve", (Bacc[:], "Bacc"), 0.0)
        for idx_, (oo, bk) in enumerate(oh_list):
            if oo != o:
                continue
            ob = n_ % 2
            n_ += 1
            dma("pool", (ohb[ob][:], "ohb%d" % ob), ohd[idx_], "coh%d" % ob)
            for h in range(8):
                stt("dve", (Bacc[:, h, :], "Bacc"), (ohb[ob][:], "ohb%d" % ob),
                    (rb_bc[:, bk * 8 + h:bk * 8 + h + 1], "rb_bc"), (Bacc[:, h, :], "Bacc"), ALU.mult, ALU.add)
        if o != 1:
            mi = 0 if o == 0 else 1
            tt("dve", (Bacc[:], "Bacc"), (Bacc[:], "Bacc"),
               (mskb[:, mi, :].unsqueeze(1).broadcast_to([128, 8, 128]), "mskb"), ALU.add)
        cp("dve", (Bt[:, o, :, :], "Bt"), (Bacc[:], "Bacc"))

    def rstd_rows(xin, xkey, s, n):
        c0 = 192 + 3 * (s % 16)
        act((junk[:, 0:n], "junk"), (xin, xkey), AF.Square, accum=(stat[:, c0:c0 + 1], "st_ss%d" % c0))
        act((stat[:, c0 + 1:c0 + 2], "st_sq%d" % c0), (stat[:, c0:c0 + 1], "st_ss%d" % c0), AF.Sqrt,
            scale=1.0 / n, bias=(eps_sb[:], "eps"))
        recip((stat[:, c0 + 2:c0 + 3], "st_r%d" % c0), (stat[:, c0 + 1:c0 + 2], "st_sq%d" % c0))
        return (stat[:, c0 + 2:c0 + 3], "st_r%d" % c0)

    def ffn_phase(tag, srcs, wA, wB, gidx, post):
        phase_reset()
        W1b = sb(tag + "W1b", [128, 8, 2 * DFF], BF16)
        W2b = sb(tag + "W2b", [128, NJ, D], BF16)
        nsub = TA // 128
        xt = [sb(tag + "xt%d" % i, [128, nsub, D], F32) for i in range(2)]
        hrow = [sb(tag + "hrow%d" % i, [128, D], BF16) for i in range(2)]
        hT = [sb(tag + "hT%d" % i, [128, 8, TA], BF16) for i in range(2)]
        h2T = sb(tag + "h2T", [128, 8, TA], BF16) if post == "A" else None
        gT = sb(tag + "gT", [128, NJ, TA], BF16)
        sa = [sb(tag + "sa%d" % i, [128, TA], F32) for i in range(2)]
        if post == "D":
            gfin_sb = sb(tag + "gfin", [128, D], F32)
            dma("sp", (gfin_sb[:], tag + "gfin"), gfin.partition_broadcast(128), "c2")
        w1v = wA.rearrange("(k p) f -> p k f", p=128)
        w2v = wB.rearrange("(j p) d -> p j d", p=128)
        JB = 11
        for jb in range(0, NJ, JB):
            nj_ = min(JB, NJ - jb)
            for hf in range(2):
                c0 = hf * DFF + jb * 128
                for k in range(8):
                    dma("pool", (W1b[:, k, c0:c0 + nj_ * 128], tag + "W1b_%d" % (jb // JB)), w1v[:, k, c0:c0 + nj_ * 128],
                        tag + "w1_%d" % (jb // JB))
        for j0 in range(0, NJ, 6):
            j1 = min(NJ, j0 + 6)
            dma("pool", (W2b[:, j0:j1, :], tag + "W2b"), w2v[:, j0:j1, :], tag + "w2")

        tiles = []
        for (src, ntok, mine, soff) in srcs:
            for t0 in range(0, ntok, TA):
                tiles.append((src, t0, mine, soff + t0))
        nt = len(tiles)

        def load_x(ti):
            src, t0, mine, so = tiles[ti]
            b = ti % 2
            dma("sp", (xt[b][:], tag + "xt%d" % b), src[t0:t0 + TA, :].rearrange("(s p) d -> p s d", p=128),
                tag + "x%d" % b)

        class Job:
            def __init__(self, xbuf, xkey, g_i, dstT, dkey, cbase):
                self.xbuf, self.xkey, self.g_i, self.dstT, self.dkey, self.cb = xbuf, xkey, g_i, dstT, dkey, cbase

            def stats(self):
                cb = self.cb
                for s in range(nsub):
                    act((junk[:], "junk"), (self.xbuf[:, s, :], self.xkey), AF.Square,
                        accum=(stat[:, cb + s:cb + s + 1], "st%d" % cb))
                act((stat[:, cb + 4:cb + 4 + nsub], "stq%d" % cb), (stat[:, cb:cb + nsub], "st%d" % cb), AF.Sqrt,
                    scale=1.0 / D, bias=(eps_sb[:], "eps"))
                recip((stat[:, cb + 8:cb + 8 + nsub], "str%d" % cb), (stat[:, cb + 4:cb + 4 + nsub], "stq%d" % cb))

            def scale(self, s):
                hb = s % 2
                cb = self.cb
                ts("dve", (hrow[hb][:], tag + "hrow%d" % hb), (self.xbuf[:, s, :], self.xkey),
                   (stat[:, cb + 8 + s:cb + 9 + s], "str%d" % cb), None, ALU.mult)

            def transp(self, s):
                hb = s % 2
                tb = 6 + (s % 2)
                pv = pbank_bf(tb).rearrange("p (k t) -> p k t", k=8)
                for k in range(8):
                    tr((pv[:, k, :], "P%d" % tb), (hrow[hb][:, k * 128:(k + 1) * 128], tag + "hrow%d" % hb))
                gb = gcol_sb[:, self.g_i, :].unsqueeze(2).broadcast_to([128, 8, 128])
                tt("dve", (self.dstT[:, :, s * 128:(s + 1) * 128], self.dkey), (pv, "P%d" % tb), (gb, "gcol"), ALU.mult)

            def run_all(self):
                self.stats()
                for s in range(nsub):
                    self.scale(s)
                    self.transp(s)

            def step(self, r):
                if r == 0:
                    self.stats()
                    self.scale(0)
                    if nsub > 1:
                        self.scale(1)
                elif 1 <= r <= nsub:
                    s = r - 1
                    self.transp(s)
                    if s + 2 < nsub:
                        self.scale(s + 2)

        def hjob(ti):
            b = ti % 2
            return Job(xt[b], tag + "xt%d" % b, gidx, hT[b], tag + "hT%d" % b, 0)

        load_x(0)
        if nt > 1:
            load_x(1)
        hjob(0).run_all()
        for ti in range(nt):
            src, t0, mine, so = tiles[ti]
            b = ti % 2
            xkey = tag + "xt%d" % b
            hk = tag + "hT%d" % b
            sched = {}
            jcur = 0
            if post == "A" and ti > 0:
                pb_ = (ti - 1) % 2
                psrc, pt0, pmine, pso = tiles[ti - 1]
                pkey = tag + "xt%d" % pb_
                if pmine:
                    dma("pool", x1s[pso:pso + TA, :].rearrange("(s p) d -> p s d", p=128), (xt[pb_][:], pkey),
                        tag + "sx%d" % pb_)
                j1 = Job(xt[pb_], pkey, 1, h2T, tag + "h2T", 16)
                for r in range(nsub + 1):
                    sched.setdefault(r, []).append((j1, r))
                sched.setdefault(nsub + 1, []).append(("store_h2", pso))
                sched.setdefault(nsub + 1, []).append(("load", ti + 1))
            elif ti + 1 < nt and ti > 0:
                sched.setdefault(0, []).append(("load", ti + 1))
            j2 = None
            if ti + 1 < nt:
                j2 = hjob(ti + 1)
                for r in range(nsub + 1):
                    sched.setdefault(12 + r, []).append((j2, r))
            for j in range(NJ):
                pa = (2 * j) % 4
                pbk = pa + 1
                wk = tag + "W1b_%d" % (j // JB)
                for k in range(8):
                    mm((pbank(pa)[:, 0:TA], "P%d" % pa), (W1b[:, k, j * 128:(j + 1) * 128], wk),
                       (hT[b][:, k, :], hk), k == 0, k == 7)
                for k in range(8):
                    mm((pbank(pbk)[:, 0:TA], "P%d" % pbk),
                       (W1b[:, k, DFF + j * 128:DFF + (j + 1) * 128], wk),
                       (hT[b][:, k, :], hk), k == 0, k == 7)
                sb_ = j % 2
                act((sa[sb_][:], tag + "sa%d" % sb_), (pbank(pa)[:, 0:TA], "P%d" % pa), AF.Silu)
                tt("dve", (gT[:, j, :], tag + "gT"), (pbank(pbk)[:, 0:TA], "P%d" % pbk),
                   (sa[sb_][:], tag + "sa%d" % sb_), ALU.mult)
                for item in sched.get(j, []):
                    if item[0] == "store_h2":
                        pso_ = item[1]
                        dma("pool", h2s[:, :, pso_:pso_ + TA].rearrange("k p t -> p k t"), (h2T[:], tag + "h2T"), tag + "sh")
                    elif item[0] == "load":
                        if item[1] < nt and item[1] >= 2:
                            load_x(item[1])
                    else:
                        item[0].step(item[1])
            for s in range(nsub):
                for hf in range(2):
                    po = 4 + ((2 * s + hf) % 2)
                    for j in range(NJ):
                        mm((pbank(po), "P%d" % po), (gT[:, j, s * 128:(s + 1) * 128], tag + "gT"),
                           (W2b[:, j, hf * 512:(hf + 1) * 512], tag + "W2b"), j == 0, j == NJ - 1)
                    stt("dve", (xt[b][:, s, hf * 512:(hf + 1) * 512], xkey), (pbank(po), "P%d" % po), 0.5,
                        (xt[b][:, s, hf * 512:(hf + 1) * 512], xkey), ALU.mult, ALU.add)
            if post == "D":
                for s in range(nsub):
                    r = rstd_rows(xt[b][:, s, :], xkey, 24 + s, D)
                    stt("dve", (xt[b][:, s, :], xkey), (xt[b][:, s, :], xkey), r,
                        (gfin_sb[:], tag + "gfin"), ALU.mult, ALU.mult)
                o = dma("pool", yout[so:so + TA, :].rearrange("(s p) d -> p s d", p=128), (xt[b][:], xkey),
                        tag + "sy%d" % b)
                final_ops.append(o)
        if post == "A":
            pb_ = (nt - 1) % 2
            psrc, pt0, pmine, pso = tiles[nt - 1]
            pkey = tag + "xt%d" % pb_
            if pmine:
                dma("pool", x1s[pso:pso + TA, :].rearrange("(s p) d -> p s d", p=128), (xt[pb_][:], pkey), tag + "sx%d" % pb_)
            Job(xt[pb_], pkey, 1, h2T, tag + "h2T", 16).run_all()
            dma("pool", h2s[:, :, pso:pso + TA].rearrange("k p t -> p k t"), (h2T[:], tag + "h2T"), tag + "sh")
        P.barrier()

    def phase_b():
        phase_reset()
        TB = 512
        winb = sb("winb", [128, 8, 3584], BF16)
        h2t = [sb("h2t%d" % i, [128, 8, TB], BF16) for i in range(2)]
        qaT_t = [sb("qaT_t%d" % i, [128, 4, TB], BF16) for i in range(2)]
        qbT_t = [sb("qbT_t%d" % i, [128, 4, TB], BF16) for i in range(2)]
        kaT_t = [sb("kaT_t%d" % i, [128, TB], BF16) for i in range(2)]
        kbT_t = [sb("kbT_t%d" % i, [128, TB], BF16) for i in range(2)]
        vA_t = [sb("vA_t%d" % i, [128, 4, 2, 128], BF16) for i in range(2)]
        vB_t = [sb("vB_t%d" % i, [128, 4, 2, 128], BF16) for i in range(2)]
        sg_t = [sb("sg_t%d" % i, [128, 16, TB], BF16) for i in range(2)]
        rC = [sb("rC%d" % i, [128, 4, 64], F32) for i in range(2)]
        rS = [sb("rS%d" % i, [128, 4, 64], F32) for i in range(2)]
        rCq = [sb("rCq%d" % i, [128, 4, 64], F32) for i in range(2)]
        rSq = [sb("rSq%d" % i, [128, 4, 64], F32) for i in range(2)]
        rCk = [sb("rCk%d" % i, [128, 4, 64], F32) for i in range(2)]
        rSk = [sb("rSk%d" % i, [128, 4, 64], F32) for i in range(2)]
        qg_sb = sb("qg_sb", [128, 4, 64], F32)
        xq = [sb("xq%d" % i, [128, 512], F32) for i in range(4)]
        xk = [sb("xk%d" % i, [128, 128], F32) for i in range(4)]
        wsq = sb("wsq", [128, 512], F32)
        wxn = [sb("wxn%d" % i, [128, 512], F32) for i in range(2)]
        wt = sb("wt", [128, 512], F32)
        wu = sb("wu", [128, 512], F32)
        qr = [sb("qr%d" % i, [128, 512], BF16) for i in range(4)]
        ksq = sb("ksq", [128, 128], F32)
        kxn = [sb("kxn%d" % i, [128, 128], F32) for i in range(2)]
        kt_ = sb("kt_", [128, 128], F32)
        ku = sb("ku", [128, 128], F32)
        kr = [sb("kr%d" % i, [128, 128], BF16) for i in range(4)]

        winv = win.rearrange("(k p) f -> p k f", p=128)
        for k in range(8):
            dma("pool", (winb[:, k, :], "winb"), winv[:, k, :], "bw")
        dma("sp", (qg_sb[:], "qg_sb"), bass.AP(qg.tensor, 0, [[0, 128], [64, 4], [1, 64]]), "c3")
        for i in range(2):
            mset("pool", (vA_t[i][:, :, 0, 64:128], "vA1_%d" % i), 1.0)
            mset("pool", (vA_t[i][:, :, 1, 0:64], "vA1_%d" % i), 1.0)
            mset("pool", (vB_t[i][:, :, 0, 64:128], "vB1_%d" % i), 1.0)
            mset("pool", (vB_t[i][:, :, 1, 0:64], "vB1_%d" % i), 1.0)

        tiles = [(t0, True) for t0 in range(0, NM, TB)] + [(t0, False) for t0 in range(NM, NTOT, TB)]
        nt = len(tiles)

        def loads(ti):
            t0, mine = tiles[ti]
            b = ti % 2
            dma("sp", (h2t[b][:], "h2t%d" % b), h2s[:, :, t0:t0 + TB].rearrange("k p t -> p k t"), "bh%d" % b)
            dma("sp", (rC[b][:], "rC%d" % b), ropeC[t0:t0 + TB, :].rearrange("(s p) c -> p s c", p=128), "brc%d" % b)
            dma("sp", (rS[b][:], "rS%d" % b), ropeS[t0:t0 + TB, :].rearrange("(s p) c -> p s c", p=128), "brs%d" % b)

        STQ = 32
        STK = 128

        loads(0)
        loads(1)
        for ti in range(nt):
            t0, mine = tiles[ti]
            b = ti % 2
            hk = "h2t%d" % b
            if mine:
                tt("dve", (rCq[b][:], "rCq%d" % b), (rC[b][:], "rC%d" % b),
                   (qg_sb[:, 0, :].unsqueeze(1).broadcast_to([128, 4, 64]), "qg_sb"), ALU.mult)
                tt("dve", (rSq[b][:], "rSq%d" % b), (rS[b][:], "rS%d" % b),
                   (qg_sb[:, 1, :].unsqueeze(1).broadcast_to([128, 4, 64]), "qg_sb"), ALU.mult)
            tt("dve", (rCk[b][:], "rCk%d" % b), (rC[b][:], "rC%d" % b),
               (qg_sb[:, 2, :].unsqueeze(1).broadcast_to([128, 4, 64]), "qg_sb"), ALU.mult)
            tt("dve", (rSk[b][:], "rSk%d" % b), (rS[b][:], "rS%d" % b),
               (qg_sb[:, 3, :].unsqueeze(1).broadcast_to([128, 4, 64]), "qg_sb"), ALU.mult)
            for s in range(4):
                p1 = s % 2
                p2 = 2 + (s % 2)
                if mine:
                    for k in range(8):
                        mm((pbank(p1), "P%d" % p1), (h2t[b][:, k, s * 128:(s + 1) * 128], hk),
                           (winb[:, k, 0:512], "winb"), k == 0, k == 7)
                    cp("act", (xq[s][:], "xq%d" % s), (pbank(p1), "P%d" % p1))
                for k in range(8):
                    mm((pbank(p2)[:, 0:384], "P%d" % p2), (h2t[b][:, k, s * 128:(s + 1) * 128], hk),
                       (winb[:, k, 512:896], "winb"), k == 0, k == 7)
                cp("act", (xk[s][:], "xk%d" % s), (pbank(p2)[:, 0:128], "P%d" % p2))
                cp("dve", (vA_t[b][:, s, 0, 0:64], "vA_t%d" % b), (pbank(p2)[:, 128:192], "P%d" % p2))
                cp("dve", (vA_t[b][:, s, 1, 64:128], "vA_t%d" % b), (pbank(p2)[:, 192:256], "P%d" % p2))
                cp("dve", (vB_t[b][:, s, 0, 0:64], "vB_t%d" % b), (pbank(p2)[:, 256:320], "P%d" % p2))
                cp("dve", (vB_t[b][:, s, 1, 64:128], "vB_t%d" % b), (pbank(p2)[:, 320:384], "P%d" % p2))
            jobs = []
            for s in range(4):
                if mine:
                    jobs.append(("q", s, 8, xq[s], "xq%d" % s, STQ + 24 * s, 8))
                jobs.append(("k", s, 2, xk[s], "xk%d" % s, STK + 8 * s, 2))
            for (kind, s, H, xb_, xk_, sc, w) in jobs:
                n = H * 64
                sqb, sqk = (wsq, "wsq") if kind == "q" else (ksq, "ksq")
                tt("dve", (sqb[:, 0:n], sqk), (xb_[:, 0:n], xk_), (xb_[:, 0:n], xk_), ALU.mult)
                red((stat[:, sc:sc + H], "st%d" % sc), (sqb[:, 0:n].rearrange("p (h d) -> p h d", h=H), sqk))
            for (kind, s, H, xb_, xk_, sc, w) in jobs:
                act((stat[:, sc + w:sc + w + H], "stq%d" % sc), (stat[:, sc:sc + H], "st%d" % sc), AF.Sqrt,
                    scale=1.0 / 64, bias=(eps_sb[:], "eps"))
            for jn, (kind, s, H, xb_, xk_, sc, w) in enumerate(jobs):
                n = H * 64
                recip((stat[:, sc + 2 * w:sc + 2 * w + H], "str%d" % sc), (stat[:, sc + w:sc + w + H], "stq%d" % sc))
                if kind == "q":
                    xnb, xnk = wxn[s % 2], "wxn%d" % (s % 2)
                    tb_, tk, ub, uk = wt, "wt", wu, "wu"
                    outb, outk = qr[s], "qr%d" % s
                    Cg, Ck_, Sg, Sk_ = rCq[b], "rCq%d" % b, rSq[b], "rSq%d" % b
                else:
                    xnb, xnk = kxn[s % 2], "kxn%d" % (s % 2)
                    tb_, tk, ub, uk = kt_, "kt_", ku, "ku"
                    outb, outk = kr[s], "kr%d" % s
                    Cg, Ck_, Sg, Sk_ = rCk[b], "rCk%d" % b, rSk[b], "rSk%d" % b
                x3 = xb_[:, 0:n].rearrange("p (h d) -> p h d", h=H)
                rb = stat[:, sc + 2 * w:sc + 2 * w + H].unsqueeze(2).broadcast_to([128, H, 64])
                xn3 = xnb[:, 0:n].rearrange("p (h d) -> p h d", h=H)
                tt("dve", (xn3, xnk), (x3, xk_), (rb, "str%d" % sc), ALU.mult)
                cb = Cg[:, s, :].unsqueeze(1).broadcast_to([128, H, 64])
                t3 = tb_[:, 0:n].rearrange("p (h d) -> p h d", h=H)
                tt("pool", (t3, tk), (xn3, xnk), (cb, Ck_), ALU.mult)
                xn5 = xnb[:, 0:n].rearrange("p (h a w j) -> p h a w j", h=H, a=2, w=2)
                u5 = ub[:, 0:n].rearrange("p (h a w j) -> p h a w j", h=H, a=2, w=2)
                s4 = Sg[:, s, :].rearrange("p (a w j) -> p a w j", a=2, w=2)
                for w_ in range(2):
                    sbc = s4[:, :, w_, :].unsqueeze(1).broadcast_to([128, H, 2, 16])
                    tt("pool", (u5[:, :, :, w_, :], uk), (xn5[:, :, :, 1 - w_, :], xnk), (sbc, Sk_), ALU.mult)
                tt("pool", (outb[:, 0:n], outk), (tb_[:, 0:n], tk), (ub[:, 0:n], uk), ALU.add)
            chunks = list(range(21)) if mine else [4]
            for n_, c in enumerate(chunks):
                pf = 4 + (n_ % 2)
                for k in range(8):
                    mm((pbank(pf), "P%d" % pf), (winb[:, k, 896 + c * 128:896 + (c + 1) * 128], "winb"),
                       (h2t[b][:, k, :], hk), k == 0, k == 7)
                if c < 4:
                    cp("act", (qbT_t[b][:, c, :], "qbT_t%d" % b), (pbank(pf), "P%d" % pf))
                elif c == 4:
                    cp("act", (kbT_t[b][:], "kbT_t%d" % b), (pbank(pf), "P%d" % pf))
                else:
                    act((sg_t[b][:, c - 5, :], "sg_t%d" % b), (pbank(pf), "P%d" % pf), AF.Sigmoid)
            for s in range(4):
                tb = 6 + (s % 2)
                pvT = pbank_bf(tb)
                if mine:
                    for i in range(4):
                        tr((pvT[:, i * 128:(i + 1) * 128], "P%d" % tb), (qr[s][:, i * 128:(i + 1) * 128], "qr%d" % s))
                tr((pvT[:, 512:640], "P%d" % tb), (kr[s][:], "kr%d" % s))
                if mine:
                    cp("dve", (qaT_t[b][:, :, s * 128:(s + 1) * 128], "qaT_t%d" % b),
                       (pvT[:, 0:512].rearrange("p (i t) -> p i t", i=4), "P%d" % tb))
                cp("dve", (kaT_t[b][:, s * 128:(s + 1) * 128], "kaT_t%d" % b), (pvT[:, 512:640], "P%d" % tb))
            dma("pool", kaTs[:, t0:t0 + TB], (kaT_t[b][:], "kaT_t%d" % b), "bska%d" % b)
            dma("pool", kbTs[:, t0:t0 + TB], (kbT_t[b][:], "kbT_t%d" % b), "bskb%d" % b)
            P.add("pool", (lambda e, o_=vaAs[t0:t0 + TB].rearrange("(s p) h d -> p s h d", p=128), i_=vA_t[b][:]:
                           e.dma_start(out=o_, in_=i_)), reads=["vA_t%d" % b, "vA1_%d" % b], writes=[],
                  dma_key="bsva%d" % b)
            P.add("pool", (lambda e, o_=vaBs[t0:t0 + TB].rearrange("(s p) h d -> p s h d", p=128), i_=vB_t[b][:]:
                           e.dma_start(out=o_, in_=i_)), reads=["vB_t%d" % b, "vB1_%d" % b], writes=[],
                  dma_key="bsvb%d" % b)
            if mine:
                dma("pool", qaTs[:, :, t0:t0 + TB].rearrange("i p t -> p i t"), (qaT_t[b][:], "qaT_t%d" % b), "bsqa%d" % b)
                dma("pool", qbTs[:, :, t0:t0 + TB].rearrange("i p t -> p i t"), (qbT_t[b][:], "qbT_t%d" % b), "bsqb%d" % b)
                dma("pool", sgs[:, :, t0:t0 + TB].rearrange("c p t -> p c t"), (sg_t[b][:], "sg_t%d" % b), "bssg%d" % b)
            if ti + 2 < nt:
                loads(ti + 2)
        P.barrier()

    def phase_c():
        phase_reset()
        TC = 512
        wbrA = sb("wbrA", [128, 4, D], BF16)
        wbrB = sb("wbrB", [128, 4, D], BF16)
        woutb = sb("woutb", [128, 8, D], BF16)
        kaTz = [sb("kaTz%d" % e, [128, SEQ_P], BF16) for e in range(2)]
        vaugA = sb("vaugA", [128, SEQ_P // 128, 2, 128], BF16)
        es_bc = sb("es_bc", [128, 8], F32)
        valid_sb = sb("valid_sb", [128, 2], F32)
        qaT_t = [sb("cqa%d" % i, [128, 4, TC], BF16) for i in range(2)]
        qbT_t = [sb("cqb%d" % i, [128, 4, TC], BF16) for i in range(2)]
        kbwz = [[sb("kbwz%d_%d" % (i, e), [128, 768], BF16) for e in range(2)] for i in range(2)]
        vbw = [sb("vbw%d" % i, [128, 6, 2, 128], BF16) for i in range(2)]
        x1t = sb("x1t", [128, 4, D], F32)
        sg_t = sb("csg", [128, 16, TC], BF16)
        pT = [sb("pT%d" % i, [128, 1024], BF16) for i in range(3)]
        pTB = [sb("pTB%d" % i, [128, 384], BF16) for i in range(2)]
        oTA = sb("oTA", [128, 4, TC], BF16)
        oTB = sb("oTB", [128, 4, TC], BF16)
        mT = sb("mT", [128, 8, TC], BF16)
        rden = [sb("rden%d" % i, [128, 512], F32) for i in range(2)]
        tmpd = sb("tmpd", [128, 512], F32)
        tmA = [sb("tmA%d" % i, [128, 512], F32) for i in range(1)]
        tmB = [sb("tmB%d" % i, [128, 512], F32) for i in range(1)]
        for (wt_, src, key) in ((wbrA, wba, "wbrA"), (wbrB, wbb, "wbrB")):
            sv = src.rearrange("(e i p) d -> e p i d", e=2, i=4, p=64)
            for e in range(2):
                dma("pool", (wt_[e * 64:(e + 1) * 64, :, :], key), sv[e], "cw" + key)
        dma("pool", (woutb[:], "woutb"), wout.rearrange("(k p) d -> p k d", p=128), "cwo")
        dma("sp", (valid_sb[:], "valid"), validd, "c4")
        dma("sp", (es_bc[:], "es_raw"), sink.partition_broadcast(128), "c5")
        act((es_bc[:], "es"), (es_bc[:], "es_raw"), AF.Exp)
        mset("pool", (kaTz[0][64:128, :], "kaT"), 0.0)
        mset("pool", (kaTz[1][0:64, :], "kaT"), 0.0)
        for i in range(2):
            mset("pool", (kbwz[i][0][64:128, :], "kbw%d" % i), 0.0)
            mset("pool", (kbwz[i][1][0:64, :], "kbw%d" % i), 0.0)
        seqs = [(0, HP, NM, SEQ_P), (HP, HS, NM + HP, SEQ_S)]
        tile_list = []
        for si, (ms, ml, ps_, sl) in enumerate(seqs):
            for t in range(ml // TC):
                tile_list.append((si, t))
        nt = len(tile_list)

        def kb_load(b, w0, w1, a0):
            n = (w1 - w0) * 128
            dma("sp", (kbwz[b][0][0:64, w0 * 128:w1 * 128], "kbw%d" % b), kbTs[0:64, a0:a0 + n], "clk%d" % b)
            dma("sp", (kbwz[b][1][64:128, w0 * 128:w1 * 128], "kbw%d" % b), kbTs[64:128, a0:a0 + n], "clk%d" % b)
            dma("sp", (vbw[b][:, w0:w1, :, :], "vbw%d" % b), vaBs[a0:a0 + n].rearrange("(c p) h d -> p c h d", p=128),
                "clv%d" % b)

        def tile_loads(ti):
            si, t = tile_list[ti]
            ms, ml, ps_, sl = seqs[si]
            b = ti % 2
            q0 = ms + t * TC
            dma("sp", (qaT_t[b][:], "cqa%d" % b), qaTs[:, :, q0:q0 + TC].rearrange("i p t -> p i t"), "cla%d" % b)
            dma("sp", (qbT_t[b][:], "cqb%d" % b), qbTs[:, :, q0:q0 + TC].rearrange("i p t -> p i t"), "clb%d" % b)
            ntl = ml // TC
            lo = 0 if t > 0 else 1
            hi = 6 if t < ntl - 1 else 5
            kb_load(b, lo, hi, q0 - 128 + lo * 128)
            if t == 0:
                kb_load(b, 0, 1, ps_ + ml - 128)
                ts("dve", (vbw[b][:, 0, :, :], "vbw%d" % b), (vbw[b][:, 0, :, :], "vbw%d" % b),
                   (valid_sb[:, 0:1], "valid"), None, ALU.mult)
            if t == ntl - 1:
                kb_load(b, 5, 6, ps_)
                ts("dve", (vbw[b][:, 5, :, :], "vbw%d" % b), (vbw[b][:, 5, :, :], "vbw%d" % b),
                   (valid_sb[:, 1:2], "valid"), None, ALU.mult)

        def seq_loads(si):
            ms, ml, ps_, sl = seqs[si]
            for (src0, dst0) in ((ms, 0), (ps_, ml)):
                for c0 in range(0, ml, 1024):
                    d0 = dst0 + c0
                    dma("sp", (kaTz[0][0:64, d0:d0 + 1024], "kaT"), kaTs[0:64, src0 + c0:src0 + c0 + 1024], "clka")
                    dma("sp", (kaTz[1][64:128, d0:d0 + 1024], "kaT"), kaTs[64:128, src0 + c0:src0 + c0 + 1024], "clka")
                    dma("sp", (vaugA[:, d0 // 128:d0 // 128 + 8, :, :], "vaugA"),
                        vaAs[src0 + c0:src0 + c0 + 1024].rearrange("(c p) h d -> p c h d", p=128), "clva")

        def normalize(psb, pkey, e, dst, dkey, es_col):
            nlo, dlo = (0, 64) if e == 0 else (64, 0)
            rb = e
            den = (psb[dlo:dlo + 64, :], pkey)
            if es_col is not None:
                ts("dve", (tmpd[dlo:dlo + 64, :], "tmpd"), den, (es_bc[dlo:dlo + 64, es_col:es_col + 1], "es"), None, ALU.add)
                den = (tmpd[dlo:dlo + 64, :], "tmpd")
            recip((rden[rb][nlo:nlo + 64, :], "rden%d" % rb), den)
            tt("dve", (dst[nlo:nlo + 64, :], dkey), (psb[nlo:nlo + 64, :], pkey), (rden[rb][nlo:nlo + 64, :], "rden%d" % rb),
               ALU.mult)

        seq_loads(0)
        tile_loads(0)
        for ti in range(nt):
            si, t = tile_list[ti]
            ms, ml, ps_, sl = seqs[si]
            b = ti % 2
            q0 = ms + t * TC
            nch = sl // 128
            if ti + 1 < nt and tile_list[ti + 1][0] == si:
                tile_loads(ti + 1)
            dma("sp", (sg_t[:], "csg"), sgs[:, :, q0:q0 + TC].rearrange("c p t -> p c t"), "clsg")
            dma("sp", (x1t[:], "x1t"), x1s[q0:q0 + TC, :].rearrange("(s p) d -> p s d", p=128), "clx1")
            qa_k = "cqa%d" % b
            qb_k = "cqb%d" % b
            heads = [(i, e) for i in range(4) for e in range(2)]
            stepsB = [(i, e, qb) for (i, e) in heads for qb in range(4)]

            def SB(n):
                i, e, qb = stepsB[n]
                h = i + 4 * e
                bank = n % 2
                v = pbank(bank)[:, 0:384].rearrange("p (o q) -> p o q", o=3)
                for o in range(3):
                    mm((v[:, o, :], "P%d" % bank), (kbwz[b][e][:, (qb + o) * 128:(qb + o + 1) * 128], "kbw%d" % b),
                       (qbT_t[b][:, i, qb * 128:(qb + 1) * 128], qb_k), True, False)
                    mm((v[:, o, :], "P%d" % bank), (ident[:], "ident"), (Bt[:, o, h, :], "Bt"), False, True)
                act((pTB[n % 2][:], "pTB%d" % (n % 2)), (pbank(bank)[:, 0:384], "P%d" % bank), AF.Exp, scale=0.125)

            def PVB(n):
                i, e, qb = stepsB[n]
                h = i + 4 * e
                hidx = heads.index((i, e))
                ob = 2 + (hidx % 2)
                pv = pTB[n % 2][:].rearrange("p (o q) -> p o q", o=3)
                for o in range(3):
                    mm((pbank(ob)[:, qb * 128:(qb + 1) * 128], "P%d" % ob), (vbw[b][:, qb + o, e, :], "vbw%d" % b),
                       (pv[:, o, :], "pTB%d" % (n % 2)), o == 0, o == 2)
                if qb == 3:
                    normalize(pbank(ob), "P%d" % ob, e, oTB[:, i, :], "oTB", h)

            SB(0)
            for n in range(len(stepsB)):
                if n + 1 < len(stepsB):
                    SB(n + 1)
                PVB(n)
            ncp = nch // 2
            stepsA = [(i, e, c2) for (i, e) in heads for c2 in range(ncp)]
            NA = len(stepsA)

            def SA(n):
                i, e, c2 = stepsA[n]
                g = n % 2
                for hh in range(2):
                    c = 2 * c2 + hh
                    bank = 2 * g + hh
                    mm((pbank(bank), "P%d" % bank), (kaTz[e][:, c * 128:(c + 1) * 128], "kaT"),
                       (qaT_t[b][:, i, :], qa_k), True, True)
                p_ = n % 3
                P.add("act", (lambda e_, o_=pT[p_][:], i_=PP[g][:]: e_.activation(out=o_, in_=i_, func=AF.Exp, scale=0.125)),
                      reads=["P%d" % (2 * g), "P%d" % (2 * g + 1)], writes=["pT%d" % p_])

            def PVA(n):
                i, e, c2 = stepsA[n]
                hidx = heads.index((i, e))
                ob = 4 + (hidx % 2)
                p_ = n % 3
                for hh in range(2):
                    c = 2 * c2 + hh
                    mm((pbank(ob), "P%d" % ob), (vaugA[:, c, e, :], "vaugA"), (pT[p_][:, hh * 512:(hh + 1) * 512], "pT%d" % p_),
                       c == 0, c == nch - 1)
                if c2 == ncp - 1:
                    normalize(pbank(ob), "P%d" % ob, e, oTA[:, i, :], "oTA", None)

            SA(0)
            for n in range(NA):
                if n + 1 < NA:
                    SA(n + 1)
                PVA(n)
            for dc in range(8):
                pya = 6 + (dc % 2)
                pyb = (dc % 2)
                for i in range(4):
                    mm((pbank(pya), "P%d" % pya), (wbrA[:, i, dc * 128:(dc + 1) * 128], "wbrA"), (oTA[:, i, :], "oTA"),
                       i == 0, i == 3)
                for i in range(4):
                    mm((pbank(pyb), "P%d" % pyb), (wbrB[:, i, dc * 128:(dc + 1) * 128], "wbrB"), (oTB[:, i, :], "oTB"),
                       i == 0, i == 3)
                x_ = 0
                tt("dve", (tmA[x_][:], "tmA%d" % x_), (pbank(pya), "P%d" % pya), (sg_t[:, dc, :], "csg"), ALU.mult)
                tt("dve", (tmB[x_][:], "tmB%d" % x_), (pbank(pyb), "P%d" % pyb), (sg_t[:, 8 + dc, :], "csg"), ALU.mult)
                tt("pool", (mT[:, dc, :], "mT"), (tmA[x_][:], "tmA%d" % x_), (tmB[x_][:], "tmB%d" % x_), ALU.add)
            for s in range(4):
                for hf in range(2):
                    pw = 2 + ((2 * s + hf) % 2)
                    for dc in range(8):
                        mm((pbank(pw), "P%d" % pw), (mT[:, dc, s * 128:(s + 1) * 128], "mT"),
                           (woutb[:, dc, hf * 512:(hf + 1) * 512], "woutb"), dc == 0, dc == 7)
                    tt("dve", (x1t[:, s, hf * 512:(hf + 1) * 512], "x1t"), (pbank(pw), "P%d" % pw),
                       (x1t[:, s, hf * 512:(hf + 1) * 512], "x1t"), ALU.add)
            dma("pool", x2s[q0:q0 + TC, :].rearrange("(s p) d -> p s d", p=128), (x1t[:], "x1t"), "csx2")
            if ti + 1 < nt and tile_list[ti + 1][0] != si:
                seq_loads(tile_list[ti + 1][0])
                tile_loads(ti + 1)
        P.barrier()

    if "A" in phases:
        ffn_phase("A", [(xm, NM, True, 0), (xp, NM, False, NM)], w1, w2, 0, "A")
    if "B" in phases:
        phase_b()
    if "C" in phases:
        phase_c()
    if "D" in phases:
        ffn_phase("D", [(x2s, NM, True, 0)], w3, w4, 2, "D")
    if not final_ops:
        final_ops.append([o for o in P.ops if o.is_dma][-1])
    P.emit(final_ops)
    return nc, P


def _prep_inputs(inputs):
    f32 = np.float32
    g = {k: np.asarray(v) for k, v in inputs.items()}
    oh_list, oh_tiles, mask_tiles = _get_bias_structure()
    w_in = g["w_in"][0]
    qa_cols = np.concatenate([np.arange(h * 64, (h + 1) * 64) for h in QPERM])
    ka_cols = np.arange(512, 640)
    va_cols = np.arange(640, 768)
    qb_cols = 768 + qa_cols
    kb_cols = np.arange(1280, 1408)
    vb_cols = np.arange(1408, 1536)
    ga_cols = np.arange(1536, 2560)
    gb_cols = np.arange(2560, 3584)
    perm = np.concatenate([qa_cols, ka_cols, va_cols, vb_cols, qb_cols, kb_cols, ga_cols, gb_cols])
    win_p = np.ascontiguousarray(w_in[:, perm])
    gc = np.stack([g["norm_ffn1"][0], g["norm_mix"][0], g["norm_ffn2"][0]], 0)
    gcols = np.ascontiguousarray(gc.reshape(3, 8, 128).transpose(2, 0, 1)).astype(f32)

    def swap(v):
        v4 = v.reshape(2, 2, 16)
        return np.ascontiguousarray(v4[:, ::-1, :]).reshape(64)

    qgv = g["q_norm_a"][0].astype(f32)
    kgv = g["k_norm_a"][0].astype(f32)
    qg = np.stack([qgv, swap(qgv), kgv, swap(kgv)], 0).astype(f32)
    common = {
        "w1": np.ascontiguousarray(g["w_ffn1_in"][0]), "w2": np.ascontiguousarray(g["w_ffn1_out"][0]),
        "w3": np.ascontiguousarray(g["w_ffn2_in"][0]), "w4": np.ascontiguousarray(g["w_ffn2_out"][0]),
        "win": win_p, "wba": np.ascontiguousarray(g["w_branch_a"][0]), "wbb": np.ascontiguousarray(g["w_branch_b"][0]),
        "wout": np.ascontiguousarray(g["w_out"][0]), "gcols": gcols, "gfin": np.ascontiguousarray(g["norm_final"]).astype(f32),
        "qg": qg, "sink": np.ascontiguousarray(g["sink_b"][0]).astype(f32),
        "relb": np.ascontiguousarray(g["rel_bias"].reshape(256)).astype(f32),
        "identd": np.eye(128, dtype=f32), "ohd": oh_tiles.astype(f32), "maskd": mask_tiles.astype(f32),
    }
    in_maps = []
    for c in range(8):
        p, r = c // 2, c % 2
        xp_ = g["x_prompt"][p]
        xs_ = g["x_sample"][p]
        mine = np.concatenate([xp_[r * HP:(r + 1) * HP], xs_[r * HS:(r + 1) * HS]], 0)
        part = np.concatenate([xp_[(1 - r) * HP:(2 - r) * HP], xs_[(1 - r) * HS:(2 - r) * HS]], 0)
        pos = np.concatenate([np.arange(r * HP, (r + 1) * HP), np.arange(r * HS, (r + 1) * HS),
                              np.arange((1 - r) * HP, (2 - r) * HP), np.arange((1 - r) * HS, (2 - r) * HS)])
        CC, SS = _rope_tables(pos)
        valid = np.zeros((128, 2), f32)
        valid[:, 0] = 1.0 if r == 1 else 0.0
        valid[:, 1] = 1.0 if r == 0 else 0.0
        m = dict(common)
        m.update({"xm": np.ascontiguousarray(mine, dtype=f32), "xp": np.ascontiguousarray(part, dtype=f32),
                  "ropeC": CC, "ropeS": SS, "validd": valid})
        in_maps.append(m)
    return in_maps


_CACHE = {}


def kernel(**inputs):
    in_maps = _prep_inputs(inputs)
    if "nc" not in _CACHE:
        _CACHE["nc"] = build_program()[0]
    nc = _CACHE["nc"]
    res = run_bass_kernel_spmd(nc, in_maps, core_ids=list(range(8)))
    yp = np.zeros((4, SEQ_P, D), np.float32)
    ys = np.zeros((4, SEQ_S, D), np.float32)
    for c in range(8):
        p, r = c // 2, c % 2
        y = np.asarray(res.results[c]["y"])
        yp[p, r * HP:(r + 1) * HP] = y[0:HP]
        ys[p, r * HS:(r + 1) * HS] = y[HP:NM]
    return (yp, ys)
```

```python
import math
import os
import numpy as np
import ml_dtypes
import concourse.bass as bass
import concourse.mybir as mybir
from concourse.bass_utils import run_bass_kernel_spmd

F32 = mybir.dt.float32
BF16 = mybir.dt.bfloat16
AF = mybir.ActivationFunctionType
ALU = mybir.AluOpType
AX = mybir.AxisListType

D = 1024
DFF = 2816
NJ = DFF // 128
SEQ_P = 8192
SEQ_S = 4096
HP = SEQ_P // 2
HS = SEQ_S // 2
NM = HP + HS
NTOT = 2 * NM
TA = 384
EPS = 1e-6
QPERM = [0, 4, 1, 5, 2, 6, 3, 7]
NEG = -30000.0

DEBUG = os.environ.get("MK_DEBUG", "") != ""


class Op:
    __slots__ = ("stream", "fn", "is_dma", "key", "deps", "signal", "val", "sem", "idx")


class Prog:
    COMPUTE = ("pe", "act", "dve", "pool")

    def __init__(self, nc):
        self.nc = nc
        self.ops = []
        self.lastw = {}
        self.readers = {}
        self.dma_cnt = {}
        self.bar_start = 0
        self.bar_deps = []
        self.bar_pending = set()

    def barrier(self):
        deps = {}
        for op in self.ops[self.bar_start:]:
            if op.is_dma:
                deps[("d", op.key)] = op
            else:
                deps[op.stream] = op
        for o in self.bar_deps:
            k = ("d", o.key) if o.is_dma else o.stream
            deps.setdefault(k, o)
        self.bar_deps = list(deps.values())
        for o in self.bar_deps:
            o.signal = True
        self.bar_start = len(self.ops)
        self.bar_pending = {"pe", "act", "dve", "pool", "sp"}

    def add(self, stream, fn, reads=(), writes=(), dma_key=None):
        op = Op()
        op.stream = stream
        op.fn = fn
        op.is_dma = dma_key is not None
        op.key = dma_key
        op.signal = op.is_dma
        op.val = 0
        op.sem = None
        op.idx = len(self.ops)
        deps = []
        for r in reads:
            w = self.lastw.get(r)
            if w is not None:
                deps.append((w, "raw"))
            if isinstance(r, str) and r[0] == "P" and r[1:].isdigit():
                rd = self.readers.get(r)
                if rd:
                    for k, o in rd.items():
                        if k != "dma" and k != stream:
                            deps.append((o, "war"))
        for r in writes:
            w = self.lastw.get(r)
            if w is not None:
                deps.append((w, "waw"))
            rd = self.readers.get(r)
            if rd:
                for k, o in rd.items():
                    if k == "dma":
                        for oo in o:
                            deps.append((oo, "war"))
                    else:
                        deps.append((o, "war"))
        keep = []
        seen = set()
        for (o, kind) in deps:
            if o is op or id(o) in seen:
                continue
            if (not o.is_dma) and (not op.is_dma) and o.stream == op.stream:
                if kind != "raw" or op.stream == "pe":
                    continue
            seen.add(id(o))
            o.signal = True
            keep.append(o)
        if stream in self.bar_pending:
            self.bar_pending.discard(stream)
            for o in self.bar_deps:
                if id(o) in seen:
                    continue
                if (not o.is_dma) and (not op.is_dma) and o.stream == op.stream == "pe":
                    continue
                seen.add(id(o))
                keep.append(o)
        op.deps = keep
        for r in reads:
            rd = self.readers.setdefault(r, {})
            if op.is_dma:
                rd.setdefault("dma", []).append(op)
            else:
                rd[op.stream] = op
        for r in writes:
            self.lastw[r] = op
            self.readers[r] = {}
        if op.is_dma:
            c = self.dma_cnt.get(dma_key, 0) + 1
            self.dma_cnt[dma_key] = c
            op.val = 16 * c
        self.ops.append(op)
        return op

    def emit(self, final_waits):
        nc = self.nc
        sems = {}

        def get_sem(name):
            if name not in sems:
                sems[name] = nc.alloc_semaphore("s_" + name.replace("/", "_"))
            return sems[name]

        LIM = 30000
        cnt = {s: 0 for s in self.COMPUTE}
        for op in self.ops:
            if op.is_dma:
                op.sem = get_sem("d_" + str(op.key))
            elif op.signal:
                c = cnt[op.stream]
                op.sem = get_sem("%s%d" % (op.stream, c // LIM))
                op.val = c % LIM + 1
                cnt[op.stream] = c + 1
        self.nsems = len(sems)
        streams = {}
        for op in self.ops:
            streams.setdefault(op.stream, []).append(op)
        fin = list(final_waits)

        def run_stream(e, ops, is_last_stream):
            waited = {}
            for op in ops:
                for d in op.deps:
                    k = id(d.sem)
                    if waited.get(k, 0) >= d.val:
                        continue
                    e.wait_ge(d.sem, d.val)
                    waited[k] = d.val
                ins = op.fn(e)
                if op.signal:
                    ins.then_inc(op.sem, 16 if op.is_dma else 1)
            if is_last_stream:
                for o in fin:
                    e.wait_ge(o.sem, o.val)

        with nc.Block() as block:
            if "sp" in streams:
                @block.sync
                def _(e):
                    run_stream(e, streams["sp"], True)
            if "pool" in streams:
                @block.gpsimd
                def _(e):
                    run_stream(e, streams["pool"], False)
            if "act" in streams:
                @block.scalar
                def _(e):
                    run_stream(e, streams["act"], False)
            if "dve" in streams:
                @block.vector
                def _(e):
                    run_stream(e, streams["dve"], False)
            if "pe" in streams:
                @block.tensor
                def _(e):
                    run_stream(e, streams["pe"], False)


def _rw(args):
    return [a[1] for a in args if a is not None and isinstance(a, tuple)]


def _ap(a):
    return a[0] if isinstance(a, tuple) else a


def _t5_bucket_np(rel):
    try:
        import jax
        import jax.numpy as jnp
        with jax.default_device(jax.devices("cpu")[0]):
            r = jnp.asarray(rel, dtype=jnp.int32)
            nb = 16
            max_exact = 8
            ret = jnp.where(r > 0, nb, 0)
            n = jnp.abs(r)
            large = max_exact + (jnp.log(jnp.maximum(n, 1).astype(jnp.float32) / max_exact)
                                 / math.log(128 / max_exact) * (nb - max_exact)).astype(jnp.int32)
            large = jnp.minimum(large, nb - 1)
            out = ret + jnp.where(n < max_exact, n, large)
            return np.asarray(out)
    except Exception:
        rel = np.asarray(rel, dtype=np.int64)
        nb, max_exact = 16, 8
        ret = np.where(rel > 0, nb, 0)
        n = np.abs(rel)
        large = max_exact + (np.log(np.maximum(n, 1).astype(np.float32) / np.float32(max_exact))
                             / np.float32(math.log(128 / max_exact)) * np.float32(nb - max_exact)).astype(np.int32)
        large = np.minimum(large, nb - 1)
        return ret + np.where(n < max_exact, n, large)


def _bias_structure():
    kk = np.arange(128)[:, None]
    qq = np.arange(128)[None, :]
    lst = []
    tiles = []
    masks = []
    for o in range(3):
        rel = (o - 1) * 128 + kk - qq
        bk = _t5_bucket_np(rel)
        valid = np.abs(rel) <= 128
        for b in range(32):
            m = (bk == b) & valid
            if m.any():
                lst.append((o, b))
                tiles.append(m.astype(np.float32) * 8.0)
        if o != 1:
            masks.append(np.where(valid, 0.0, NEG).astype(np.float32))
    return lst, np.stack(tiles), np.stack(masks)


_OH_LIST, _OH_TILES, _MASK_TILES = None, None, None


def _get_bias_structure():
    global _OH_LIST, _OH_TILES, _MASK_TILES
    if _OH_LIST is None:
        _OH_LIST, _OH_TILES, _MASK_TILES = _bias_structure()
    return _OH_LIST, _OH_TILES, _MASK_TILES


def _rope_tables(pos):
    pos = np.asarray(pos, dtype=np.int64)
    row = (pos // 64).astype(np.float32)
    col = (pos % 64).astype(np.float32)
    inv = (np.float32(10000.0) ** (-np.arange(0, 32, 2, dtype=np.float32) / np.float32(32))).astype(np.float32)
    CC = np.zeros((len(pos), 64), np.float32)
    SS = np.zeros((len(pos), 64), np.float32)
    for a, p in enumerate((row, col)):
        ang = (p[:, None] * inv[None, :]).astype(np.float32)
        c = np.cos(ang).astype(np.float32)
        s = np.sin(ang).astype(np.float32)
        CC[:, a * 32:a * 32 + 16] = c
        CC[:, a * 32 + 16:a * 32 + 32] = c
        SS[:, a * 32:a * 32 + 16] = -s
        SS[:, a * 32 + 16:a * 32 + 32] = s
    return CC, SS


SB_BASE = 16512 + 64
SB_LIMIT = 228480


def build_program(phases="ABCD"):
    oh_list, _, _ = _get_bias_structure()
    NOH = len(oh_list)
    nc = bass.Bass("TRN2", target_bir_lowering=False)
    P = Prog(nc)

    def din(name, shape, dt=F32):
        return nc.dram_tensor(name, list(shape), dt, kind="ExternalInput").ap()

    def dscr(name, shape, dt):
        kind = "ExternalOutput" if DEBUG else "Internal"
        return nc.dram_tensor(name, list(shape), dt, kind=kind).ap()

    xm = din("xm", [NM, D])
    xp = din("xp", [NM, D])
    w1 = din("w1", [D, 2 * DFF])
    w2 = din("w2", [DFF, D])
    w3 = din("w3", [D, 2 * DFF])
    w4 = din("w4", [DFF, D])
    win = din("win", [D, 3584])
    wba = din("wba", [512, D])
    wbb = din("wbb", [512, D])
    wout = din("wout", [D, D])
    gcols = din("gcols", [128, 3, 8])
    gfin = din("gfin", [D])
    qg = din("qg", [4, 64])
    sink = din("sink", [8])
    relb = din("relb", [256])
    ropeC = din("ropeC", [NTOT, 64])
    ropeS = din("ropeS", [NTOT, 64])
    identd = din("identd", [128, 128])
    ohd = din("ohd", [NOH, 128, 128], BF16)
    maskd = din("maskd", [2, 128, 128])
    validd = din("validd", [128, 2])
    yout = nc.dram_tensor("y", [NM, D], F32, kind="ExternalOutput").ap()

    x1s = dscr("x1s", [NM, D], F32)
    h2s = dscr("h2s", [8, 128, NTOT], BF16)
    kaTs = dscr("kaTs", [128, NTOT], BF16)
    kbTs = dscr("kbTs", [128, NTOT], BF16)
    vaAs = dscr("vaAs", [NTOT, 2, 128], BF16)
    vaBs = dscr("vaBs", [NTOT, 2, 128], BF16)
    qaTs = dscr("qaTs", [4, 128, NM], BF16)
    qbTs = dscr("qbTs", [4, 128, NM], BF16)
    sgs = dscr("sgs", [16, 128, NM], BF16)
    x2s = dscr("x2s", [NM, D], F32)

    PP = [nc.alloc_psum_tensor("pp%d" % i, [128, 1024], F32) for i in range(4)]

    def pbank(i):
        return PP[i // 2][:, (i % 2) * 512:(i % 2 + 1) * 512]

    def pbank_bf(i):
        return PP[i // 2][:].bitcast(BF16)[:, (i % 2) * 1024:(i % 2 + 1) * 1024]

    arena = {"pers": SB_BASE, "cur": SB_BASE, "n": 0}

    def sb(name, shape, dt, persistent=False):
        nbytes = int(np.prod(shape[1:])) * (4 if dt == F32 else 2)
        nbytes = (nbytes + 63) // 64 * 64
        off = arena["cur"]
        assert off + nbytes <= SB_LIMIT, ("SBUF overflow", name, off, nbytes)
        arena["cur"] = off + nbytes
        arena["n"] += 1
        h = nc.alloc_sbuf_tensor_at("%s_%d" % (name, arena["n"]), list(shape), dt, offset=off)
        if persistent:
            arena["pers"] = arena["cur"]
        return h

    def phase_reset():
        arena["cur"] = arena["pers"]

    top = {"cur": SB_LIMIT - 1152}

    def sb_top(name, shape, dt):
        nbytes = int(np.prod(shape[1:])) * (4 if dt == F32 else 2)
        nbytes = (nbytes + 63) // 64 * 64
        top["cur"] -= nbytes
        arena["n"] += 1
        return nc.alloc_sbuf_tensor_at("%s_%d" % (name, arena["n"]), list(shape), dt, offset=top["cur"])

    ident = sb("ident", [128, 128], BF16, True)
    gcol_sb = sb("gcol_sb", [128, 3, 8], F32, True)
    eps_sb = sb("eps_sb", [128, 1], F32, True)
    junk = sb("junk", [128, 1024], BF16, True)
    stat = sb("stat", [128, 256], F32, True)

    def dma(stream, out, in_, key):
        o, i = _ap(out), _ap(in_)
        return P.add(stream, lambda e: e.dma_start(out=o, in_=i), reads=_rw([in_]), writes=_rw([out]),
                     dma_key=key)

    def mm(out, lhsT, rhs, start, stop):
        o, l, r = _ap(out), _ap(lhsT), _ap(rhs)
        return P.add("pe", lambda e: e.matmul(o, lhsT=l, rhs=r, start=start, stop=stop),
                     reads=_rw([lhsT, rhs]), writes=_rw([out]))

    def tr(out, in_):
        o, i = _ap(out), _ap(in_)
        idn = ident[:]
        return P.add("pe", lambda e: e.transpose(o, i, idn), reads=_rw([in_]) + ["ident"], writes=_rw([out]))

    def act(out, in_, func, scale=1.0, bias=None, accum=None):
        o, i = _ap(out), _ap(in_)
        sc = _ap(scale) if isinstance(scale, tuple) else scale
        kw = {}
        if bias is not None:
            kw["bias"] = _ap(bias)
        if accum is not None:
            kw["accum_out"] = _ap(accum)
        rd = _rw([in_]) + (_rw([scale]) if isinstance(scale, tuple) else []) + (_rw([bias]) if bias is not None else [])
        wr = _rw([out]) + (_rw([accum]) if accum is not None else [])
        return P.add("act", lambda e: e.activation(out=o, in_=i, func=func, scale=sc, **kw), reads=rd, writes=wr)

    def tt(eng, out, in0, in1, op):
        o, a, b = _ap(out), _ap(in0), _ap(in1)
        return P.add(eng, lambda e: e.tensor_tensor(out=o, in0=a, in1=b, op=op), reads=_rw([in0, in1]),
                     writes=_rw([out]))

    def ts(eng, out, in0, s1, s2, op0, op1=None):
        o, a = _ap(out), _ap(in0)
        s1a = _ap(s1) if isinstance(s1, tuple) else s1
        s2a = _ap(s2) if isinstance(s2, tuple) else s2
        rd = _rw([in0]) + (_rw([s1]) if isinstance(s1, tuple) else []) + (_rw([s2]) if isinstance(s2, tuple) else [])
        if op1 is None:
            return P.add(eng, lambda e: e.tensor_scalar(out=o, in0=a, scalar1=s1a, scalar2=None, op0=op0),
                         reads=rd, writes=_rw([out]))
        return P.add(eng, lambda e: e.tensor_scalar(out=o, in0=a, scalar1=s1a, scalar2=s2a, op0=op0, op1=op1),
                     reads=rd, writes=_rw([out]))

    def stt(eng, out, in0, scalar, in1, op0, op1):
        o, a, b = _ap(out), _ap(in0), _ap(in1)
        s = _ap(scalar) if isinstance(scalar, tuple) else scalar
        rd = _rw([in0, in1]) + (_rw([scalar]) if isinstance(scalar, tuple) else [])
        return P.add(eng, lambda e: e.scalar_tensor_tensor(out=o, in0=a, scalar=s, in1=b, op0=op0, op1=op1),
                     reads=rd, writes=_rw([out]))

    def cp(eng, out, in_):
        o, i = _ap(out), _ap(in_)
        if eng == "act":
            return P.add("act", lambda e: e.copy(out=o, in_=i), reads=_rw([in_]), writes=_rw([out]))
        return P.add(eng, lambda e: e.tensor_copy(out=o, in_=i), reads=_rw([in_]), writes=_rw([out]))

    def recip(out, in_):
        o, i = _ap(out), _ap(in_)
        return P.add("dve", lambda e: e.reciprocal(out=o, in_=i), reads=_rw([in_]), writes=_rw([out]))

    def red(out, in_):
        o, i = _ap(out), _ap(in_)
        return P.add("dve", lambda e: e.tensor_reduce(out=o, in_=i, axis=AX.X, op=ALU.add), reads=_rw([in_]),
                     writes=_rw([out]))

    def mset(eng, out, val):
        o = _ap(out)
        return P.add(eng, lambda e: e.memset(o, val), reads=[], writes=_rw([out]))

    final_ops = []

    dma("pool", (ident[:], "ident"), identd, "c0")
    dma("sp", (gcol_sb[:], "gcol"), gcols, "c1")
    mset("dve", (eps_sb[:], "eps"), EPS)

    Bt = sb("Bt", [128, 3, 8, 128], BF16, True)
    Bacc = sb_top("Bacc", [128, 8, 128], F32)
    rb_bc = sb_top("rb_bc", [128, 256], F32)
    ohb = [sb_top("ohb%d" % i, [128, 128], BF16) for i in range(2)]
    mskb = sb_top("mskb", [128, 2, 128], F32)
    dma("sp", (rb_bc[:], "rb_bc"), relb.partition_broadcast(128), "c6")
    dma("sp", (mskb[:], "mskb"), maskd.rearrange("m p q -> p m q"), "c7")
    n_ = 0
    for o in range(3):
        mset("dve", (Bacc[:], "Bacc"), 0.0)
        for idx_, (oo, bk) in enumerate(oh_list):
            if oo != o:
                continue
            ob = n_ % 2
            n_ += 1
            dma("sp", (ohb[ob][:], "ohb%d" % ob), ohd[idx_], "coh%d" % ob)
            for h in range(8):
                stt("dve", (Bacc[:, h, :], "Bacc"), (ohb[ob][:], "ohb%d" % ob),
                    (rb_bc[:, bk * 8 + h:bk * 8 + h + 1], "rb_bc"), (Bacc[:, h, :], "Bacc"), ALU.mult, ALU.add)
        if o != 1:
            mi = 0 if o == 0 else 1
            tt("dve", (Bacc[:], "Bacc"), (Bacc[:], "Bacc"),
               (mskb[:, mi, :].unsqueeze(1).broadcast_to([128, 8, 128]), "mskb"), ALU.add)
        cp("dve", (Bt[:, o, :, :], "Bt"), (Bacc[:], "Bacc"))

    def rstd_rows(xin, xkey, s, n):
        c0 = 192 + 3 * (s % 16)
        act((junk[:, 0:n], "junk"), (xin, xkey), AF.Square, accum=(stat[:, c0:c0 + 1], "st_ss%d" % c0))
        act((stat[:, c0 + 1:c0 + 2], "st_sq%d" % c0), (stat[:, c0:c0 + 1], "st_ss%d" % c0), AF.Sqrt,
            scale=1.0 / n, bias=(eps_sb[:], "eps"))
        recip((stat[:, c0 + 2:c0 + 3], "st_r%d" % c0), (stat[:, c0 + 1:c0 + 2], "st_sq%d" % c0))
        return (stat[:, c0 + 2:c0 + 3], "st_r%d" % c0)

    def ffn_phase(tag, srcs, wA, wB, gidx, post):
        phase_reset()
        W1b = sb(tag + "W1b", [128, 8, 2 * DFF], BF16)
        W2b = sb(tag + "W2b", [128, NJ, D], BF16)
        nsub = TA // 128
        xt = [sb(tag + "xt%d" % i, [128, nsub, D], F32) for i in range(2)]
        hrow = [sb(tag + "hrow%d" % i, [128, D], BF16) for i in range(2)]
        hT = [sb(tag + "hT%d" % i, [128, 8, TA], BF16) for i in range(2)]
        h2T = sb(tag + "h2T", [128, 8, TA], BF16) if post == "A" else None
        gT = sb(tag + "gT", [128, NJ, TA], BF16)
        sa = [sb(tag + "sa%d" % i, [128, TA], F32) for i in range(2)]
        if post == "D":
            gfin_sb = sb(tag + "gfin", [128, D], F32)
            dma("sp", (gfin_sb[:], tag + "gfin"), gfin.partition_broadcast(128), "c2")
        w1v = wA.rearrange("(k p) f -> p k f", p=128)
        w2v = wB.rearrange("(j p) d -> p j d", p=128)
        JB = 11
        for jb in range(0, NJ, JB):
            nj_ = min(JB, NJ - jb)
            for hf in range(2):
                c0 = hf * DFF + jb * 128
                for k in range(8):
                    dma("pool", (W1b[:, k, c0:c0 + nj_ * 128], tag + "W1b_%d" % (jb // JB)), w1v[:, k, c0:c0 + nj_ * 128],
                        tag + "w1_%d" % (jb // JB))
        for j0 in range(0, NJ, 6):
            j1 = min(NJ, j0 + 6)
            dma("pool", (W2b[:, j0:j1, :], tag + "W2b"), w2v[:, j0:j1, :], tag + "w2")

        tiles = []
        for (src, ntok, mine, soff) in srcs:
            for t0 in range(0, ntok, TA):
                tiles.append((src, t0, mine, soff + t0))
        nt = len(tiles)

        def load_x(ti):
            src, t0, mine, so = tiles[ti]
            b = ti % 2
            dma("sp", (xt[b][:], tag + "xt%d" % b), src[t0:t0 + TA, :].rearrange("(s p) d -> p s d", p=128),
                tag + "x%d" % b)

        class Job:
            def __init__(self, xbuf, xkey, g_i, dstT, dkey, cbase):
                self.xbuf, self.xkey, self.g_i, self.dstT, self.dkey, self.cb = xbuf, xkey, g_i, dstT, dkey, cbase

            def stats(self):
                cb = self.cb
                for s in range(nsub):
                    act((junk[:], "junk"), (self.xbuf[:, s, :], self.xkey), AF.Square,
                        accum=(stat[:, cb + s:cb + s + 1], "st%d" % cb))
                act((stat[:, cb + 4:cb + 4 + nsub], "stq%d" % cb), (stat[:, cb:cb + nsub], "st%d" % cb), AF.Sqrt,
                    scale=1.0 / D, bias=(eps_sb[:], "eps"))
                recip((stat[:, cb + 8:cb + 8 + nsub], "str%d" % cb), (stat[:, cb + 4:cb + 4 + nsub], "stq%d" % cb))

            def scale(self, s):
                hb = s % 2
                cb = self.cb
                ts("dve", (hrow[hb][:], tag + "hrow%d" % hb), (self.xbuf[:, s, :], self.xkey),
                   (stat[:, cb + 8 + s:cb + 9 + s], "str%d" % cb), None, ALU.mult)

            def transp(self, s):
                hb = s % 2
                tb = 6 + (s % 2)
                pv = pbank_bf(tb).rearrange("p (k t) -> p k t", k=8)
                for k in range(8):
                    tr((pv[:, k, :], "P%d" % tb), (hrow[hb][:, k * 128:(k + 1) * 128], tag + "hrow%d" % hb))
                gb = gcol_sb[:, self.g_i, :].unsqueeze(2).broadcast_to([128, 8, 128])
                tt("dve", (self.dstT[:, :, s * 128:(s + 1) * 128], self.dkey), (pv, "P%d" % tb), (gb, "gcol"), ALU.mult)

            def run_all(self):
                self.stats()
                for s in range(nsub):
                    self.scale(s)
                    self.transp(s)

            def step(self, r):
                if r == 0:
                    self.stats()
                    self.scale(0)
                    if nsub > 1:
                        self.scale(1)
                elif 1 <= r <= nsub:
                    s = r - 1
                    self.transp(s)
                    if s + 2 < nsub:
                        self.scale(s + 2)

        def hjob(ti):
            b = ti % 2
            return Job(xt[b], tag + "xt%d" % b, gidx, hT[b], tag + "hT%d" % b, 0)

        load_x(0)
        if nt > 1:
            load_x(1)
        hjob(0).run_all()
        for ti in range(nt):
            src, t0, mine, so = tiles[ti]
            b = ti % 2
            xkey = tag + "xt%d" % b
            hk = tag + "hT%d" % b
            sched = {}
            jcur = 0
            if post == "A" and ti > 0:
                pb_ = (ti - 1) % 2
                psrc, pt0, pmine, pso = tiles[ti - 1]
                pkey = tag + "xt%d" % pb_
                if pmine:
                    dma("pool", x1s[pso:pso + TA, :].rearrange("(s p) d -> p s d", p=128), (xt[pb_][:], pkey),
                        tag + "sx%d" % pb_)
                j1 = Job(xt[pb_], pkey, 1, h2T, tag + "h2T", 16)
                for r in range(nsub + 1):
                    sched.setdefault(r, []).append((j1, r))
                sched.setdefault(nsub + 1, []).append(("store_h2", pso))
                sched.setdefault(nsub + 1, []).append(("load", ti + 1))
            elif ti + 1 < nt and ti > 0:
                sched.setdefault(0, []).append(("load", ti + 1))
            j2 = None
            if ti + 1 < nt:
                j2 = hjob(ti + 1)
                for r in range(nsub + 1):
                    sched.setdefault(12 + r, []).append((j2, r))
            for j in range(NJ):
                pa = (2 * j) % 4
                pbk = pa + 1
                wk = tag + "W1b_%d" % (j // JB)
                for k in range(8):
                    mm((pbank(pa)[:, 0:TA], "P%d" % pa), (W1b[:, k, j * 128:(j + 1) * 128], wk),
                       (hT[b][:, k, :], hk), k == 0, k == 7)
                for k in range(8):
                    mm((pbank(pbk)[:, 0:TA], "P%d" % pbk),
                       (W1b[:, k, DFF + j * 128:DFF + (j + 1) * 128], wk),
                       (hT[b][:, k, :], hk), k == 0, k == 7)
                sb_ = j % 2
                act((sa[sb_][:], tag + "sa%d" % sb_), (pbank(pa)[:, 0:TA], "P%d" % pa), AF.Silu)
                tt("dve", (gT[:, j, :], tag + "gT"), (pbank(pbk)[:, 0:TA], "P%d" % pbk),
                   (sa[sb_][:], tag + "sa%d" % sb_), ALU.mult)
                for item in sched.get(j, []):
                    if item[0] == "store_h2":
                        pso_ = item[1]
                        dma("pool", h2s[:, :, pso_:pso_ + TA].rearrange("k p t -> p k t"), (h2T[:], tag + "h2T"), tag + "sh")
                    elif item[0] == "load":
                        if item[1] < nt and item[1] >= 2:
                            load_x(item[1])
                    else:
                        item[0].step(item[1])
            for s in range(nsub):
                for hf in range(2):
                    po = 4 + ((2 * s + hf) % 2)
                    for j in range(NJ):
                        mm((pbank(po), "P%d" % po), (gT[:, j, s * 128:(s + 1) * 128], tag + "gT"),
                           (W2b[:, j, hf * 512:(hf + 1) * 512], tag + "W2b"), j == 0, j == NJ - 1)
                    stt("dve", (xt[b][:, s, hf * 512:(hf + 1) * 512], xkey), (pbank(po), "P%d" % po), 0.5,
                        (xt[b][:, s, hf * 512:(hf + 1) * 512], xkey), ALU.mult, ALU.add)
            if post == "D":
                for s in range(nsub):
                    r = rstd_rows(xt[b][:, s, :], xkey, 24 + s, D)
                    stt("dve", (xt[b][:, s, :], xkey), (xt[b][:, s, :], xkey), r,
                        (gfin_sb[:], tag + "gfin"), ALU.mult, ALU.mult)
                o = dma("pool", yout[so:so + TA, :].rearrange("(s p) d -> p s d", p=128), (xt[b][:], xkey),
                        tag + "sy%d" % b)
                final_ops.append(o)
        if post == "A":
            pb_ = (nt - 1) % 2
            psrc, pt0, pmine, pso = tiles[nt - 1]
            pkey = tag + "xt%d" % pb_
            if pmine:
                dma("pool", x1s[pso:pso + TA, :].rearrange("(s p) d -> p s d", p=128), (xt[pb_][:], pkey), tag + "sx%d" % pb_)
            Job(xt[pb_], pkey, 1, h2T, tag + "h2T", 16).run_all()
            dma("pool", h2s[:, :, pso:pso + TA].rearrange("k p t -> p k t"), (h2T[:], tag + "h2T"), tag + "sh")
        P.barrier()

    def phase_b():
        phase_reset()
        TB = 512
        winb = sb("winb", [128, 8, 3584], BF16)
        h2t = [sb("h2t%d" % i, [128, 8, TB], BF16) for i in range(2)]
        qaT_t = [sb("qaT_t%d" % i, [128, 4, TB], BF16) for i in range(2)]
        qbT_t = [sb("qbT_t%d" % i, [128, 4, TB], BF16) for i in range(2)]
        kaT_t = [sb("kaT_t%d" % i, [128, TB], BF16) for i in range(2)]
        kbT_t = [sb("kbT_t%d" % i, [128, TB], BF16) for i in range(2)]
        vA_t = [sb("vA_t%d" % i, [128, 4, 2, 128], BF16) for i in range(2)]
        vB_t = [sb("vB_t%d" % i, [128, 4, 2, 128], BF16) for i in range(2)]
        sg_t = [sb("sg_t%d" % i, [128, 16, TB], BF16) for i in range(2)]
        rC = [sb("rC%d" % i, [128, 4, 64], F32) for i in range(2)]
        rS = [sb("rS%d" % i, [128, 4, 64], F32) for i in range(2)]
        rCq = [sb("rCq%d" % i, [128, 4, 64], F32) for i in range(2)]
        rSq = [sb("rSq%d" % i, [128, 4, 64], F32) for i in range(2)]
        rCk = [sb("rCk%d" % i, [128, 4, 64], F32) for i in range(2)]
        rSk = [sb("rSk%d" % i, [128, 4, 64], F32) for i in range(2)]
        qg_sb = sb("qg_sb", [128, 4, 64], F32)
        xq = [sb("xq%d" % i, [128, 512], F32) for i in range(4)]
        xk = [sb("xk%d" % i, [128, 128], F32) for i in range(4)]
        wsq = sb("wsq", [128, 512], F32)
        wxn = [sb("wxn%d" % i, [128, 512], F32) for i in range(2)]
        wt = sb("wt", [128, 512], F32)
        wu = sb("wu", [128, 512], F32)
        qr = [sb("qr%d" % i, [128, 512], BF16) for i in range(4)]
        ksq = sb("ksq", [128, 128], F32)
        kxn = [sb("kxn%d" % i, [128, 128], F32) for i in range(2)]
        kt_ = sb("kt_", [128, 128], F32)
        ku = sb("ku", [128, 128], F32)
        kr = [sb("kr%d" % i, [128, 128], BF16) for i in range(4)]

        winv = win.rearrange("(k p) f -> p k f", p=128)
        for k in range(8):
            dma("pool", (winb[:, k, :], "winb"), winv[:, k, :], "bw")
        dma("sp", (qg_sb[:], "qg_sb"), bass.AP(qg.tensor, 0, [[0, 128], [64, 4], [1, 64]]), "c3")
        for i in range(2):
            mset("pool", (vA_t[i][:, :, 0, 64:128], "vA1_%d" % i), 1.0)
            mset("pool", (vA_t[i][:, :, 1, 0:64], "vA1_%d" % i), 1.0)
            mset("pool", (vB_t[i][:, :, 0, 64:128], "vB1_%d" % i), 1.0)
            mset("pool", (vB_t[i][:, :, 1, 0:64], "vB1_%d" % i), 1.0)

        tiles = [(t0, True) for t0 in range(0, NM, TB)] + [(t0, False) for t0 in range(NM, NTOT, TB)]
        nt = len(tiles)

        def loads(ti):
            t0, mine = tiles[ti]
            b = ti % 2
            dma("sp", (h2t[b][:], "h2t%d" % b), h2s[:, :, t0:t0 + TB].rearrange("k p t -> p k t"), "bh%d" % b)
            dma("sp", (rC[b][:], "rC%d" % b), ropeC[t0:t0 + TB, :].rearrange("(s p) c -> p s c", p=128), "brc%d" % b)
            dma("sp", (rS[b][:], "rS%d" % b), ropeS[t0:t0 + TB, :].rearrange("(s p) c -> p s c", p=128), "brs%d" % b)

        STQ = 32
        STK = 128

        loads(0)
        loads(1)
        for ti in range(nt):
            t0, mine = tiles[ti]
            b = ti % 2
            hk = "h2t%d" % b
            if mine:
                tt("dve", (rCq[b][:], "rCq%d" % b), (rC[b][:], "rC%d" % b),
                   (qg_sb[:, 0, :].unsqueeze(1).broadcast_to([128, 4, 64]), "qg_sb"), ALU.mult)
                tt("dve", (rSq[b][:], "rSq%d" % b), (rS[b][:], "rS%d" % b),
                   (qg_sb[:, 1, :].unsqueeze(1).broadcast_to([128, 4, 64]), "qg_sb"), ALU.mult)
            tt("dve", (rCk[b][:], "rCk%d" % b), (rC[b][:], "rC%d" % b),
               (qg_sb[:, 2, :].unsqueeze(1).broadcast_to([128, 4, 64]), "qg_sb"), ALU.mult)
            tt("dve", (rSk[b][:], "rSk%d" % b), (rS[b][:], "rS%d" % b),
               (qg_sb[:, 3, :].unsqueeze(1).broadcast_to([128, 4, 64]), "qg_sb"), ALU.mult)
            for s in range(4):
                p1 = s % 2
                p2 = 2 + (s % 2)
                if mine:
                    for k in range(8):
                        mm((pbank(p1), "P%d" % p1), (h2t[b][:, k, s * 128:(s + 1) * 128], hk),
                           (winb[:, k, 0:512], "winb"), k == 0, k == 7)
                    cp("act", (xq[s][:], "xq%d" % s), (pbank(p1), "P%d" % p1))
                for k in range(8):
                    mm((pbank(p2)[:, 0:384], "P%d" % p2), (h2t[b][:, k, s * 128:(s + 1) * 128], hk),
                       (winb[:, k, 512:896], "winb"), k == 0, k == 7)
                cp("act", (xk[s][:], "xk%d" % s), (pbank(p2)[:, 0:128], "P%d" % p2))
                cp("dve", (vA_t[b][:, s, 0, 0:64], "vA_t%d" % b), (pbank(p2)[:, 128:192], "P%d" % p2))
                cp("dve", (vA_t[b][:, s, 1, 64:128], "vA_t%d" % b), (pbank(p2)[:, 192:256], "P%d" % p2))
                cp("dve", (vB_t[b][:, s, 0, 0:64], "vB_t%d" % b), (pbank(p2)[:, 256:320], "P%d" % p2))
                cp("dve", (vB_t[b][:, s, 1, 64:128], "vB_t%d" % b), (pbank(p2)[:, 320:384], "P%d" % p2))
            jobs = []
            for s in range(4):
                if mine:
                    jobs.append(("q", s, 8, xq[s], "xq%d" % s, STQ + 24 * s, 8))
                jobs.append(("k", s, 2, xk[s], "xk%d" % s, STK + 8 * s, 2))
            for (kind, s, H, xb_, xk_, sc, w) in jobs:
                n = H * 64
                sqb, sqk = (wsq, "wsq") if kind == "q" else (ksq, "ksq")
                tt("dve", (sqb[:, 0:n], sqk), (xb_[:, 0:n], xk_), (xb_[:, 0:n], xk_), ALU.mult)
                red((stat[:, sc:sc + H], "st%d" % sc), (sqb[:, 0:n].rearrange("p (h d) -> p h d", h=H), sqk))
            for (kind, s, H, xb_, xk_, sc, w) in jobs:
                act((stat[:, sc + w:sc + w + H], "stq%d" % sc), (stat[:, sc:sc + H], "st%d" % sc), AF.Sqrt,
                    scale=1.0 / 64, bias=(eps_sb[:], "eps"))
            for jn, (kind, s, H, xb_, xk_, sc, w) in enumerate(jobs):
                n = H * 64
                recip((stat[:, sc + 2 * w:sc + 2 * w + H], "str%d" % sc), (stat[:, sc + w:sc + w + H], "stq%d" % sc))
                if kind == "q":
                    xnb, xnk = wxn[s % 2], "wxn%d" % (s % 2)
                    tb_, tk, ub, uk = wt, "wt", wu, "wu"
                    outb, outk = qr[s], "qr%d" % s
                    Cg, Ck_, Sg, Sk_ = rCq[b], "rCq%d" % b, rSq[b], "rSq%d" % b
                else:
                    xnb, xnk = kxn[s % 2], "kxn%d" % (s % 2)
                    tb_, tk, ub, uk = kt_, "kt_", ku, "ku"
                    outb, outk = kr[s], "kr%d" % s
                    Cg, Ck_, Sg, Sk_ = rCk[b], "rCk%d" % b, rSk[b], "rSk%d" % b
                x3 = xb_[:, 0:n].rearrange("p (h d) -> p h d", h=H)
                rb = stat[:, sc + 2 * w:sc + 2 * w + H].unsqueeze(2).broadcast_to([128, H, 64])
                xn3 = xnb[:, 0:n].rearrange("p (h d) -> p h d", h=H)
                tt("dve", (xn3, xnk), (x3, xk_), (rb, "str%d" % sc), ALU.mult)
                cb = Cg[:, s, :].unsqueeze(1).broadcast_to([128, H, 64])
                t3 = tb_[:, 0:n].rearrange("p (h d) -> p h d", h=H)
                tt("pool", (t3, tk), (xn3, xnk), (cb, Ck_), ALU.mult)
                xn5 = xnb[:, 0:n].rearrange("p (h a w j) -> p h a w j", h=H, a=2, w=2)
                u5 = ub[:, 0:n].rearrange("p (h a w j) -> p h a w j", h=H, a=2, w=2)
                s4 = Sg[:, s, :].rearrange("p (a w j) -> p a w j", a=2, w=2)
                for w_ in range(2):
                    sbc = s4[:, :, w_, :].unsqueeze(1).broadcast_to([128, H, 2, 16])
                    tt("pool", (u5[:, :, :, w_, :], uk), (xn5[:, :, :, 1 - w_, :], xnk), (sbc, Sk_), ALU.mult)
                tt("pool", (outb[:, 0:n], outk), (tb_[:, 0:n], tk), (ub[:, 0:n], uk), ALU.add)
            chunks = list(range(21)) if mine else [4]
            for n_, c in enumerate(chunks):
                pf = 4 + (n_ % 2)
                for k in range(8):
                    mm((pbank(pf), "P%d" % pf), (winb[:, k, 896 + c * 128:896 + (c + 1) * 128], "winb"),
                       (h2t[b][:, k, :], hk), k == 0, k == 7)
                if c < 4:
                    cp("act", (qbT_t[b][:, c, :], "qbT_t%d" % b), (pbank(pf), "P%d" % pf))
                elif c == 4:
                    cp("act", (kbT_t[b][:], "kbT_t%d" % b), (pbank(pf), "P%d" % pf))
                else:
                    act((sg_t[b][:, c - 5, :], "sg_t%d" % b), (pbank(pf), "P%d" % pf), AF.Sigmoid)
            for s in range(4):
                tb = 6 + (s % 2)
                pvT = pbank_bf(tb)
                if mine:
                    for i in range(4):
                        tr((pvT[:, i * 128:(i + 1) * 128], "P%d" % tb), (qr[s][:, i * 128:(i + 1) * 128], "qr%d" % s))
                tr((pvT[:, 512:640], "P%d" % tb), (kr[s][:], "kr%d" % s))
                if mine:
                    cp("dve", (qaT_t[b][:, :, s * 128:(s + 1) * 128], "qaT_t%d" % b),
                       (pvT[:, 0:512].rearrange("p (i t) -> p i t", i=4), "P%d" % tb))
                cp("dve", (kaT_t[b][:, s * 128:(s + 1) * 128], "kaT_t%d" % b), (pvT[:, 512:640], "P%d" % tb))
            dma("pool", kaTs[:, t0:t0 + TB], (kaT_t[b][:], "kaT_t%d" % b), "bska%d" % b)
            dma("pool", kbTs[:, t0:t0 + TB], (kbT_t[b][:], "kbT_t%d" % b), "bskb%d" % b)
            P.add("pool", (lambda e, o_=vaAs[t0:t0 + TB].rearrange("(s p) h d -> p s h d", p=128), i_=vA_t[b][:]:
                           e.dma_start(out=o_, in_=i_)), reads=["vA_t%d" % b, "vA1_%d" % b], writes=[],
                  dma_key="bsva%d" % b)
            P.add("pool", (lambda e, o_=vaBs[t0:t0 + TB].rearrange("(s p) h d -> p s h d", p=128), i_=vB_t[b][:]:
                           e.dma_start(out=o_, in_=i_)), reads=["vB_t%d" % b, "vB1_%d" % b], writes=[],
                  dma_key="bsvb%d" % b)
            if mine:
                dma("pool", qaTs[:, :, t0:t0 + TB].rearrange("i p t -> p i t"), (qaT_t[b][:], "qaT_t%d" % b), "bsqa%d" % b)
                dma("pool", qbTs[:, :, t0:t0 + TB].rearrange("i p t -> p i t"), (qbT_t[b][:], "qbT_t%d" % b), "bsqb%d" % b)
                dma("pool", sgs[:, :, t0:t0 + TB].rearrange("c p t -> p c t"), (sg_t[b][:], "sg_t%d" % b), "bssg%d" % b)
            if ti + 2 < nt:
                loads(ti + 2)
        P.barrier()

    def phase_c():
        phase_reset()
        TC = 512
        wbrA = sb("wbrA", [128, 4, D], BF16)
        wbrB = sb("wbrB", [128, 4, D], BF16)
        woutb = sb("woutb", [128, 8, D], BF16)
        kaTz = [sb("kaTz%d" % e, [128, SEQ_P], BF16) for e in range(2)]
        vaugA = sb("vaugA", [128, SEQ_P // 128, 2, 128], BF16)
        es_bc = sb("es_bc", [128, 8], F32)
        valid_sb = sb("valid_sb", [128, 2], F32)
        qaT_t = [sb("cqa%d" % i, [128, 4, TC], BF16) for i in range(2)]
        qbT_t = [sb("cqb%d" % i, [128, 4, TC], BF16) for i in range(2)]
        kbwz = [[sb("kbwz%d_%d" % (i, e), [128, 768], BF16) for e in range(2)] for i in range(2)]
        vbw = [sb("vbw%d" % i, [128, 6, 2, 128], BF16) for i in range(2)]
        x1t = sb("x1t", [128, 4, D], F32)
        sg_t = sb("csg", [128, 16, TC], BF16)
        pT = [sb("pT%d" % i, [128, 1024], BF16) for i in range(3)]
        pTB = [sb("pTB%d" % i, [128, 384], BF16) for i in range(2)]
        oTA = sb("oTA", [128, 4, TC], BF16)
        oTB = sb("oTB", [128, 4, TC], BF16)
        mT = sb("mT", [128, 8, TC], BF16)
        rden = [sb("rden%d" % i, [128, 512], F32) for i in range(2)]
        tmpd = sb("tmpd", [128, 512], F32)
        tmA = [sb("tmA%d" % i, [128, 512], F32) for i in range(1)]
        tmB = [sb("tmB%d" % i, [128, 512], F32) for i in range(1)]
        for (wt_, src, key) in ((wbrA, wba, "wbrA"), (wbrB, wbb, "wbrB")):
            sv = src.rearrange("(e i p) d -> e p i d", e=2, i=4, p=64)
            for e in range(2):
                dma("pool", (wt_[e * 64:(e + 1) * 64, :, :], key), sv[e], "cw" + key)
        dma("pool", (woutb[:], "woutb"), wout.rearrange("(k p) d -> p k d", p=128), "cwo")
        dma("sp", (valid_sb[:], "valid"), validd, "c4")
        dma("sp", (es_bc[:], "es_raw"), sink.partition_broadcast(128), "c5")
        act((es_bc[:], "es"), (es_bc[:], "es_raw"), AF.Exp)
        mset("pool", (kaTz[0][64:128, :], "kaT"), 0.0)
        mset("pool", (kaTz[1][0:64, :], "kaT"), 0.0)
        for i in range(2):
            mset("pool", (kbwz[i][0][64:128, :], "kbw%d" % i), 0.0)
            mset("pool", (kbwz[i][1][0:64, :], "kbw%d" % i), 0.0)
        seqs = [(0, HP, NM, SEQ_P), (HP, HS, NM + HP, SEQ_S)]
        tile_list = []
        for si, (ms, ml, ps_, sl) in enumerate(seqs):
            for t in range(ml // TC):
                tile_list.append((si, t))
        nt = len(tile_list)

        def kb_load(b, w0, w1, a0):
            n = (w1 - w0) * 128
            dma("sp", (kbwz[b][0][0:64, w0 * 128:w1 * 128], "kbw%d" % b), kbTs[0:64, a0:a0 + n], "clk%d" % b)
            dma("sp", (kbwz[b][1][64:128, w0 * 128:w1 * 128], "kbw%d" % b), kbTs[64:128, a0:a0 + n], "clk%d" % b)
            dma("sp", (vbw[b][:, w0:w1, :, :], "vbw%d" % b), vaBs[a0:a0 + n].rearrange("(c p) h d -> p c h d", p=128),
                "clv%d" % b)

        def tile_loads(ti):
            si, t = tile_list[ti]
            ms, ml, ps_, sl = seqs[si]
            b = ti % 2
            q0 = ms + t * TC
            dma("sp", (qaT_t[b][:], "cqa%d" % b), qaTs[:, :, q0:q0 + TC].rearrange("i p t -> p i t"), "cla%d" % b)
            dma("sp", (qbT_t[b][:], "cqb%d" % b), qbTs[:, :, q0:q0 + TC].rearrange("i p t -> p i t"), "clb%d" % b)
            ntl = ml // TC
            lo = 0 if t > 0 else 1
            hi = 6 if t < ntl - 1 else 5
            kb_load(b, lo, hi, q0 - 128 + lo * 128)
            if t == 0:
                kb_load(b, 0, 1, ps_ + ml - 128)
                ts("dve", (vbw[b][:, 0, :, :], "vbw%d" % b), (vbw[b][:, 0, :, :], "vbw%d" % b),
                   (valid_sb[:, 0:1], "valid"), None, ALU.mult)
            if t == ntl - 1:
                kb_load(b, 5, 6, ps_)
                ts("dve", (vbw[b][:, 5, :, :], "vbw%d" % b), (vbw[b][:, 5, :, :], "vbw%d" % b),
                   (valid_sb[:, 1:2], "valid"), None, ALU.mult)

        def seq_loads(si):
            ms, ml, ps_, sl = seqs[si]
            for (src0, dst0) in ((ms, 0), (ps_, ml)):
                for c0 in range(0, ml, 1024):
                    d0 = dst0 + c0
                    dma("sp", (kaTz[0][0:64, d0:d0 + 1024], "kaT"), kaTs[0:64, src0 + c0:src0 + c0 + 1024], "clka")
                    dma("sp", (kaTz[1][64:128, d0:d0 + 1024], "kaT"), kaTs[64:128, src0 + c0:src0 + c0 + 1024], "clka")
                    dma("sp", (vaugA[:, d0 // 128:d0 // 128 + 8, :, :], "vaugA"),
                        vaAs[src0 + c0:src0 + c0 + 1024].rearrange("(c p) h d -> p c h d", p=128), "clva")

        def normalize(psb, pkey, e, dst, dkey, es_col):
            nlo, dlo = (0, 64) if e == 0 else (64, 0)
            rb = e
            den = (psb[dlo:dlo + 64, :], pkey)
            if es_col is not None:
                ts("dve", (tmpd[dlo:dlo + 64, :], "tmpd"), den, (es_bc[dlo:dlo + 64, es_col:es_col + 1], "es"), None, ALU.add)
                den = (tmpd[dlo:dlo + 64, :], "tmpd")
            recip((rden[rb][nlo:nlo + 64, :], "rden%d" % rb), den)
            tt("dve", (dst[nlo:nlo + 64, :], dkey), (psb[nlo:nlo + 64, :], pkey), (rden[rb][nlo:nlo + 64, :], "rden%d" % rb),
               ALU.mult)

        seq_loads(0)
        tile_loads(0)
        for ti in range(nt):
            si, t = tile_list[ti]
            ms, ml, ps_, sl = seqs[si]
            b = ti % 2
            q0 = ms + t * TC
            nch = sl // 128
            if ti + 1 < nt and tile_list[ti + 1][0] == si:
                tile_loads(ti + 1)
            dma("sp", (sg_t[:], "csg"), sgs[:, :, q0:q0 + TC].rearrange("c p t -> p c t"), "clsg")
            dma("sp", (x1t[:], "x1t"), x1s[q0:q0 + TC, :].rearrange("(s p) d -> p s d", p=128), "clx1")
            qa_k = "cqa%d" % b
            qb_k = "cqb%d" % b
            heads = [(i, e) for i in range(4) for e in range(2)]
            stepsB = [(i, e, qb) for (i, e) in heads for qb in range(4)]

            def SB(n):
                i, e, qb = stepsB[n]
                h = i + 4 * e
                bank = n % 2
                v = pbank(bank)[:, 0:384].rearrange("p (o q) -> p o q", o=3)
                for o in range(3):
                    mm((v[:, o, :], "P%d" % bank), (kbwz[b][e][:, (qb + o) * 128:(qb + o + 1) * 128], "kbw%d" % b),
                       (qbT_t[b][:, i, qb * 128:(qb + 1) * 128], qb_k), True, False)
                    mm((v[:, o, :], "P%d" % bank), (ident[:], "ident"), (Bt[:, o, h, :], "Bt"), False, True)
                act((pTB[n % 2][:], "pTB%d" % (n % 2)), (pbank(bank)[:, 0:384], "P%d" % bank), AF.Exp, scale=0.125)

            def PVB(n):
                i, e, qb = stepsB[n]
                h = i + 4 * e
                hidx = heads.index((i, e))
                ob = 2 + (hidx % 2)
                pv = pTB[n % 2][:].rearrange("p (o q) -> p o q", o=3)
                for o in range(3):
                    mm((pbank(ob)[:, qb * 128:(qb + 1) * 128], "P%d" % ob), (vbw[b][:, qb + o, e, :], "vbw%d" % b),
                       (pv[:, o, :], "pTB%d" % (n % 2)), o == 0, o == 2)
                if qb == 3:
                    normalize(pbank(ob), "P%d" % ob, e, oTB[:, i, :], "oTB", h)

            SB(0)
            for n in range(len(stepsB)):
                if n + 1 < len(stepsB):
                    SB(n + 1)
                PVB(n)
            ncp = nch // 2
            stepsA = [(i, e, c2) for (i, e) in heads for c2 in range(ncp)]
            NA = len(stepsA)

            def SA(n):
                i, e, c2 = stepsA[n]
                g = n % 3
                for hh in range(2):
                    c = 2 * c2 + hh
                    bank = 2 * g + hh
                    mm((pbank(bank), "P%d" % bank), (kaTz[e][:, c * 128:(c + 1) * 128], "kaT"),
                       (qaT_t[b][:, i, :], qa_k), True, True)
                p_ = n % 3
                P.add("act", (lambda e_, o_=pT[p_][:], i_=PP[g][:]: e_.activation(out=o_, in_=i_, func=AF.Exp, scale=0.125)),
                      reads=["P%d" % (2 * g), "P%d" % (2 * g + 1)], writes=["pT%d" % p_])

            def PVA(n):
                i, e, c2 = stepsA[n]
                hidx = heads.index((i, e))
                ob = 6 + (hidx % 2)
                p_ = n % 3
                for hh in range(2):
                    c = 2 * c2 + hh
                    mm((pbank(ob), "P%d" % ob), (vaugA[:, c, e, :], "vaugA"), (pT[p_][:, hh * 512:(hh + 1) * 512], "pT%d" % p_),
                       c == 0, c == nch - 1)
                if c2 == ncp - 1:
                    normalize(pbank(ob), "P%d" % ob, e, oTA[:, i, :], "oTA", None)

            for n in range(min(3, NA)):
                SA(n)
            for n in range(NA):
                PVA(n)
                if n + 3 < NA:
                    SA(n + 3)
            for dc in range(8):
                pya = 4 + (dc % 2)
                pyb = (dc % 2)
                for i in range(4):
                    mm((pbank(pya), "P%d" % pya), (wbrA[:, i, dc * 128:(dc + 1) * 128], "wbrA"), (oTA[:, i, :], "oTA"),
                       i == 0, i == 3)
                for i in range(4):
                    mm((pbank(pyb), "P%d" % pyb), (wbrB[:, i, dc * 128:(dc + 1) * 128], "wbrB"), (oTB[:, i, :], "oTB"),
                       i == 0, i == 3)
                x_ = 0
                tt("dve", (tmA[x_][:], "tmA%d" % x_), (pbank(pya), "P%d" % pya), (sg_t[:, dc, :], "csg"), ALU.mult)
                tt("dve", (tmB[x_][:], "tmB%d" % x_), (pbank(pyb), "P%d" % pyb), (sg_t[:, 8 + dc, :], "csg"), ALU.mult)
                tt("pool", (mT[:, dc, :], "mT"), (tmA[x_][:], "tmA%d" % x_), (tmB[x_][:], "tmB%d" % x_), ALU.add)
            for s in range(4):
                for hf in range(2):
                    pw = 2 + ((2 * s + hf) % 2)
                    for dc in range(8):
                        mm((pbank(pw), "P%d" % pw), (mT[:, dc, s * 128:(s + 1) * 128], "mT"),
                           (woutb[:, dc, hf * 512:(hf + 1) * 512], "woutb"), dc == 0, dc == 7)
                    tt("dve", (x1t[:, s, hf * 512:(hf + 1) * 512], "x1t"), (pbank(pw), "P%d" % pw),
                       (x1t[:, s, hf * 512:(hf + 1) * 512], "x1t"), ALU.add)
            dma("pool", x2s[q0:q0 + TC, :].rearrange("(s p) d -> p s d", p=128), (x1t[:], "x1t"), "csx2")
            if ti + 1 < nt and tile_list[ti + 1][0] != si:
                seq_loads(tile_list[ti + 1][0])
                tile_loads(ti + 1)
        P.barrier()

    if "A" in phases:
        ffn_phase("A", [(xm, NM, True, 0), (xp, NM, False, NM)], w1, w2, 0, "A")
    if "B" in phases:
        phase_b()
    if "C" in phases:
        phase_c()
    if "D" in phases:
        ffn_phase("D", [(x2s, NM, True, 0)], w3, w4, 2, "D")
    if not final_ops:
        final_ops.append([o for o in P.ops if o.is_dma][-1])
    P.emit(final_ops)
    return nc, P


def _prep_inputs(inputs):
    f32 = np.float32
    g = {k: np.asarray(v) for k, v in inputs.items()}
    oh_list, oh_tiles, mask_tiles = _get_bias_structure()
    w_in = g["w_in"][0]
    qa_cols = np.concatenate([np.arange(h * 64, (h + 1) * 64) for h in QPERM])
    ka_cols = np.arange(512, 640)
    va_cols = np.arange(640, 768)
    qb_cols = 768 + qa_cols
    kb_cols = np.arange(1280, 1408)
    vb_cols = np.arange(1408, 1536)
    ga_cols = np.arange(1536, 2560)
    gb_cols = np.arange(2560, 3584)
    perm = np.concatenate([qa_cols, ka_cols, va_cols, vb_cols, qb_cols, kb_cols, ga_cols, gb_cols])
    win_p = np.ascontiguousarray(w_in[:, perm])
    gc = np.stack([g["norm_ffn1"][0], g["norm_mix"][0], g["norm_ffn2"][0]], 0)
    gcols = np.ascontiguousarray(gc.reshape(3, 8, 128).transpose(2, 0, 1)).astype(f32)

    def swap(v):
        v4 = v.reshape(2, 2, 16)
        return np.ascontiguousarray(v4[:, ::-1, :]).reshape(64)

    qgv = g["q_norm_a"][0].astype(f32)
    kgv = g["k_norm_a"][0].astype(f32)
    qg = np.stack([qgv, swap(qgv), kgv, swap(kgv)], 0).astype(f32)
    common = {
        "w1": np.ascontiguousarray(g["w_ffn1_in"][0]), "w2": np.ascontiguousarray(g["w_ffn1_out"][0]),
        "w3": np.ascontiguousarray(g["w_ffn2_in"][0]), "w4": np.ascontiguousarray(g["w_ffn2_out"][0]),
        "win": win_p, "wba": np.ascontiguousarray(g["w_branch_a"][0]), "wbb": np.ascontiguousarray(g["w_branch_b"][0]),
        "wout": np.ascontiguousarray(g["w_out"][0]), "gcols": gcols, "gfin": np.ascontiguousarray(g["norm_final"]).astype(f32),
        "qg": qg, "sink": np.ascontiguousarray(g["sink_b"][0]).astype(f32),
        "relb": np.ascontiguousarray(g["rel_bias"].reshape(256)).astype(f32),
        "identd": np.eye(128, dtype=f32), "ohd": oh_tiles.astype(ml_dtypes.bfloat16), "maskd": mask_tiles.astype(f32),
    }
    in_maps = []
    for c in range(8):
        p, r = c // 2, c % 2
        xp_ = g["x_prompt"][p]
        xs_ = g["x_sample"][p]
        mine = np.concatenate([xp_[r * HP:(r + 1) * HP], xs_[r * HS:(r + 1) * HS]], 0)
        part = np.concatenate([xp_[(1 - r) * HP:(2 - r) * HP], xs_[(1 - r) * HS:(2 - r) * HS]], 0)
        pos = np.concatenate([np.arange(r * HP, (r + 1) * HP), np.arange(r * HS, (r + 1) * HS),
                              np.arange((1 - r) * HP, (2 - r) * HP), np.arange((1 - r) * HS, (2 - r) * HS)])
        CC, SS = _rope_tables(pos)
        valid = np.zeros((128, 2), f32)
        valid[:, 0] = 1.0 if r == 1 else 0.0
        valid[:, 1] = 1.0 if r == 0 else 0.0
        m = dict(common)
        m.update({"xm": np.ascontiguousarray(mine, dtype=f32), "xp": np.ascontiguousarray(part, dtype=f32),
                  "ropeC": CC, "ropeS": SS, "validd": valid})
        in_maps.append(m)
    return in_maps


_CACHE = {}


def kernel(**inputs):
    in_maps = _prep_inputs(inputs)
    if "nc" not in _CACHE:
        _CACHE["nc"] = build_program()[0]
    nc = _CACHE["nc"]
    res = run_bass_kernel_spmd(nc, in_maps, core_ids=list(range(8)))
    yp = np.zeros((4, SEQ_P, D), np.float32)
    ys = np.zeros((4, SEQ_S, D), np.float32)
    for c in range(8):
        p, r = c // 2, c % 2
        y = np.asarray(res.results[c]["y"])
        yp[p, r * HP:(r + 1) * HP] = y[0:HP]
        ys[p, r * HS:(r + 1) * HS] = y[HP:NM]
    return (yp, ys)
```
